# Optimizing a Trainium2 kernel written in Bass

```python
import jax, jax.numpy as jnp
from jax import lax
import numpy as np

D_MODEL = 1024
BATCH = 4
SEQ = 4096
DEPTH = 1

D_MIX = D_MODEL
SWA_WIDTH = D_MIX // 2
SWA_HEAD_DIM = 64
SWA_Q_HEADS = SWA_WIDTH // SWA_HEAD_DIM
SWA_KV_HEADS = 2
SWA_GROUP = SWA_Q_HEADS // SWA_KV_HEADS
WINDOW = 128
BLOCK = 128
ROPE_THETA = 500000.0
ROT_DIM = SWA_HEAD_DIM // 4
GLA_WIDTH = D_MIX - SWA_WIDTH
GLA_HEADS = 4
GLA_DK = GLA_WIDTH // 2 // GLA_HEADS
GLA_DV = GLA_WIDTH // GLA_HEADS
GLA_RANK = 16
GLA_TAU = 16.0
GLA_CHUNK = 64
IN_SPLITS = (
    SWA_Q_HEADS * SWA_HEAD_DIM,
    SWA_KV_HEADS * SWA_HEAD_DIM,
    SWA_KV_HEADS * SWA_HEAD_DIM,
    SWA_WIDTH,
    GLA_HEADS * GLA_DK,
    GLA_HEADS * GLA_DK,
    GLA_HEADS * GLA_DV,
    GLA_WIDTH,
    GLA_RANK,
)
D_IN_PROJ = sum(IN_SPLITS)
EPS = 1e-5
ALPHA = (2 * DEPTH) ** 0.25
BETA = (8 * DEPTH) ** -0.25

kernel_name = 'hymba_swa_sink_gla_deepnorm'


def split_cols(t, sizes):
    out, start = [], 0
    for s in sizes:
        out.append(t[..., start:start + s])
        start += s
    return out


def partial_rope(t, pos):
    half = ROT_DIM // 2
    inv_freq = ROPE_THETA ** (-jnp.arange(half, dtype=jnp.float32) / half)
    ang = pos.astype(jnp.float32)[..., None] * inv_freq
    cos = jnp.cos(ang)[:, :, None, :].astype(t.dtype)
    sin = jnp.sin(ang)[:, :, None, :].astype(t.dtype)
    t1 = t[..., :half]
    t2 = t[..., half:ROT_DIM]
    return jnp.concatenate([t1 * cos - t2 * sin, t2 * cos + t1 * sin, t[..., ROT_DIM:]], axis=-1)


def sliding_window_attention(q, k, v, sinks):
    B, S = q.shape[0], q.shape[1]
    nb = S // BLOCK
    qb = q.reshape(B, nb, BLOCK, SWA_KV_HEADS, SWA_GROUP, SWA_HEAD_DIM)

    def with_prev(t):
        tb = t.reshape(B, nb, BLOCK, SWA_KV_HEADS, SWA_HEAD_DIM)
        prev = jnp.concatenate([jnp.zeros_like(tb[:, :1]), tb[:, :-1]], axis=1)
        return jnp.concatenate([prev, tb], axis=2)

    kb = with_prev(k)
    vb = with_prev(v)
    scale = SWA_HEAD_DIM ** -0.5
    scores = jnp.einsum('bnqhgd,bnkhd->bnhgqk', qb, kb).astype(jnp.float32) * scale
    qi = jnp.arange(BLOCK)[:, None]
    ki = jnp.arange(2 * BLOCK)[None, :]
    dist = qi + BLOCK - ki
    in_window = (dist >= 0) & (dist < WINDOW)
    has_prev = (jnp.arange(nb)[:, None, None] > 0) | (ki >= BLOCK)[None]
    mask = in_window[None] & has_prev
    scores = jnp.where(mask[None, :, None, None], scores, -jnp.inf)
    sink = sinks.astype(jnp.float32).reshape(SWA_KV_HEADS, SWA_GROUP)[None, None, :, :, None, None]
    m = jnp.maximum(scores.max(axis=-1, keepdims=True), sink)
    p = jnp.exp(scores - m)
    denom = p.sum(axis=-1, keepdims=True) + jnp.exp(sink - m)
    probs = (p / denom).astype(v.dtype)
    out = jnp.einsum('bnhgqk,bnkhd->bnqhgd', probs, vb)
    return out.reshape(B, S, SWA_Q_HEADS * SWA_HEAD_DIM)


def gla_chunked(q, k, v, log_a):
    B, S = q.shape[0], q.shape[1]
    C = GLA_CHUNK
    nc = S // C

    def chunks(t):
        return t.reshape(B, nc, C, t.shape[2], t.shape[3]).astype(jnp.float32)

    qc = chunks(q) * (GLA_DK ** -0.5)
    kc = chunks(k)
    vc = chunks(v)
    b = jnp.cumsum(chunks(log_a), axis=2)
    b_last = b[:, :, -1:]
    q_dec = qc * jnp.exp(b)
    k_inv = kc * jnp.exp(-b)
    k_to_end = kc * jnp.exp(b_last - b)
    causal = jnp.tril(jnp.ones((C, C), dtype=bool))
    attn = jnp.einsum('bnihd,bnjhd->bnhij', q_dec, k_inv)
    attn = jnp.where(causal, attn, 0.0)
    o_intra = jnp.einsum('bnhij,bnjhv->bnihv', attn, vc)
    upd = jnp.einsum('bnjhd,bnjhv->bnhdv', k_to_end, vc)
    decay = jnp.exp(b_last[:, :, 0])

    def step(state, inp):
        dec, u = inp
        return state * dec[..., None] + u, state

    init = jnp.zeros((B, GLA_HEADS, GLA_DK, GLA_DV), jnp.float32)
    _, s_prev = lax.scan(step, init, (jnp.swapaxes(decay, 0, 1), jnp.swapaxes(upd, 0, 1)))
    s_prev = jnp.swapaxes(s_prev, 0, 1)
    o_inter = jnp.einsum('bnihd,bnhdv->bnihv', q_dec, s_prev)
    return (o_intra + o_inter).reshape(B, S, GLA_HEADS, GLA_DV)


def layer_norm(t, g, b):
    tf = t.astype(jnp.float32)
    mu = tf.mean(axis=-1, keepdims=True)
    var = jnp.square(tf - mu).mean(axis=-1, keepdims=True)
    return ((tf - mu) * lax.rsqrt(var + EPS) * g.astype(jnp.float32) + b.astype(jnp.float32)).astype(t.dtype)


def setup_inputs(seed: int = 0) -> dict:
    key = jax.random.key(seed)
    ks = jax.random.split(key, 12)
    x = jax.random.normal(ks[0], (BATCH, SEQ, D_MODEL), jnp.float32)
    offset = jax.random.randint(ks[1], (BATCH, 1), 0, 1024, dtype=jnp.int32)
    positions = offset + jnp.arange(SEQ, dtype=jnp.int32)[None, :]
    w_in = jax.random.normal(ks[2], (DEPTH, D_MODEL, D_IN_PROJ), jnp.float32) * D_MODEL ** -0.5
    starts = np.cumsum((0,) + IN_SPLITS)
    col_scale = np.ones((D_IN_PROJ,), np.float32)
    col_scale[starts[2]:starts[3]] = BETA
    col_scale[starts[6]:starts[7]] = BETA
    w_in = w_in * jnp.asarray(col_scale)
    gla_w_gate_up = jax.random.normal(ks[3], (DEPTH, GLA_RANK, GLA_HEADS * GLA_DK), jnp.float32) * GLA_RANK ** -0.5
    gla_b_gate = 0.01 * jax.random.normal(ks[4], (DEPTH, GLA_HEADS * GLA_DK), jnp.float32)
    attn_sinks = 0.5 * jax.random.normal(ks[5], (DEPTH, SWA_Q_HEADS), jnp.float32)
    gla_norm_w = 1.0 + 0.01 * jax.random.normal(ks[6], (DEPTH, GLA_DV), jnp.float32)
    w_out = jax.random.normal(ks[7], (DEPTH, D_MIX, D_MODEL), jnp.float32) * (D_MIX ** -0.5) * BETA
    ln_g = 1.0 + 0.01 * jax.random.normal(ks[8], (DEPTH, D_MODEL), jnp.float32)
    ln_b = 0.01 * jax.random.normal(ks[9], (DEPTH, D_MODEL), jnp.float32)
    return {'x': x, 'positions': positions, 'w_in': w_in, 'gla_w_gate_up': gla_w_gate_up,
            'gla_b_gate': gla_b_gate, 'attn_sinks': attn_sinks, 'gla_norm_w': gla_norm_w,
            'w_out': w_out, 'ln_g': ln_g, 'ln_b': ln_b}


def reference(x, positions, w_in, gla_w_gate_up, gla_b_gate, attn_sinks, gla_norm_w, w_out, ln_g, ln_b):
    B, S = x.shape[0], x.shape[1]
    for layer in range(DEPTH):
        proj = jnp.einsum('bsd,de->bse', x, w_in[layer])
        q_a, k_a, v_a, g_a, q_b, k_b, v_b, g_b, r_b = split_cols(proj, IN_SPLITS)
        q_a = partial_rope(q_a.reshape(B, S, SWA_Q_HEADS, SWA_HEAD_DIM), positions)
        k_a = partial_rope(k_a.reshape(B, S, SWA_KV_HEADS, SWA_HEAD_DIM), positions)
        v_a = v_a.reshape(B, S, SWA_KV_HEADS, SWA_HEAD_DIM)
        out_a = sliding_window_attention(q_a, k_a, v_a, attn_sinks[layer]) * jax.nn.silu(g_a)
        gate_logit = jnp.einsum('bsr,re->bse', r_b, gla_w_gate_up[layer]) + gla_b_gate[layer]
        log_a = jax.nn.log_sigmoid(gate_logit.astype(jnp.float32)) / GLA_TAU
        o_b = gla_chunked(q_b.reshape(B, S, GLA_HEADS, GLA_DK),
                          k_b.reshape(B, S, GLA_HEADS, GLA_DK),
                          v_b.reshape(B, S, GLA_HEADS, GLA_DV),
                          log_a.reshape(B, S, GLA_HEADS, GLA_DK))
        o_b = o_b * lax.rsqrt(jnp.mean(jnp.square(o_b), axis=-1, keepdims=True) + EPS) * gla_norm_w[layer].astype(jnp.float32)
        out_b = o_b.reshape(B, S, GLA_WIDTH).astype(x.dtype) * jax.nn.silu(g_b)
        mix = jnp.einsum('bse,ed->bsd', jnp.concatenate([out_a, out_b], axis=-1), w_out[layer])
        x = layer_norm(ALPHA * x + mix, ln_g[layer], ln_b[layer])
    return x
```

```python
import math
from contextlib import ExitStack

import numpy as np
import concourse.bass as bass
import concourse.mybir as mybir
from concourse.bass_utils import run_bass_kernel_spmd

F32 = mybir.dt.float32
BF16 = mybir.dt.bfloat16
I32 = mybir.dt.int32
AF = mybir.ActivationFunctionType
ALU = mybir.AluOpType
AX = mybir.AxisListType

D = 1024
NCORES = 8
SEQ = 4096
BATCH = 4
HALF = SEQ // 2
EPS = 1e-5
ALPHA = 2.0 ** 0.25
SCALE_A = 64 ** -0.5
SCALE_B = 64 ** -0.5
NEG = -30000.0
CAND_LIMIT = 130
RING_A, RING_B = 3, 2
Y_LONG = True
PREFIX_WIDE = True
PREFIX_BANKS = 8
SLACK_NS = 0.0
XT_ENG = "dve"
TWO_PI = 2.0 * math.pi
PI_SAFE = 3.1415925
CW1 = 6.28125
CW2 = float(np.float32(TWO_PI - CW1))
CW3 = float(TWO_PI - CW1 - float(np.float32(TWO_PI - CW1)))


class Ins:
    __slots__ = ("eng", "fn", "deps", "sig", "seq", "dma_tag", "dma_cnt", "idx", "cost", "odeps", "fin", "lat")

    def __init__(self, eng, fn, dma_tag):
        self.eng = eng
        self.fn = fn
        self.idx = 0
        self.cost = 100.0
        self.lat = 0.0
        self.odeps = set()
        self.fin = 0.0
        self.deps = set()
        self.sig = False
        self.seq = 0
        self.dma_tag = dma_tag
        self.dma_cnt = 0


def _region(ap):
    t = ap.tensor
    name = t.name
    esz = mybir.dt.size(ap.dtype)
    dims = [(int(s) * esz, int(c)) for s, c in ap.ap]
    off = int(ap.offset) * esz
    space = str(getattr(t, "space", ""))
    kind = "sb"
    cls = type(t).__name__
    if "DRam" in cls or "Dram" in cls or "DRAM" in cls:
        kind = "dram"
    elif "PSum" in cls or "Psum" in cls or "PSUM" in cls:
        kind = "ps"
    if kind == "dram":
        ext = sum((c - 1) * abs(s) for s, c in dims)
        return (name, kind, 0, 1, off, off + ext + esz)
    pstep = dims[0][0]
    if pstep <= 0:
        return (name, kind, 0, 128, 0, 1 << 40)
    p0 = off // pstep
    f0 = off % pstep
    ext = sum((c - 1) * abs(s) for s, c in dims[1:])
    return (name, kind, p0, p0 + dims[0][1], f0, f0 + ext + esz)


def _ovl(a, b):
    return a[2] < b[3] and b[2] < a[3] and a[4] < b[5] and b[4] < a[5]


def _contains(a, b):
    return a[2] <= b[2] and b[3] <= a[3] and a[4] <= b[4] and b[5] <= a[5]


class Prog:
    ENG = ["pe", "act", "dve", "pool", "sp"]

    def __init__(self, nc):
        self.nc = nc
        self.streams = {e: [] for e in self.ENG}
        self.wr = {}
        self.rd = {}
        self.ps_last = {}
        self.dma_counts = {}
        self.group_tags = set()
        self.n = 0
        self.last_dma = {}

    def add(self, eng, fn, outs, ins, dma_tag=None, cost=100.0, lat=0.0, extra=()):
        I = Ins(eng, fn, dma_tag)
        I.idx = self.n
        self.n += 1
        I.cost = cost
        I.lat = lat
        if dma_tag is not None:
            prevq = self.last_dma.get(eng)
            if prevq is not None:
                I.odeps.add(prevq)
            self.last_dma[eng] = I
        if dma_tag is not None:
            self.dma_counts[dma_tag] = self.dma_counts.get(dma_tag, 0) + 1
            I.dma_cnt = self.dma_counts[dma_tag]
        rin = [_region(a) for a in ins]
        rout = [_region(a) for a in outs]
        deps = set()
        for r in rin:
            for rr, J in self.wr.get(r[0], ()):
                if _ovl(r, rr):
                    deps.add(J)
        for r in rout:
            for rr, J in self.wr.get(r[0], ()):
                if _ovl(r, rr):
                    deps.add(J)
            for rr, J in self.rd.get(r[0], {}).values():
                if _ovl(r, rr):
                    deps.add(J)
        for r in rin + rout:
            if r[1] == "ps":
                last = self.ps_last.setdefault(r[0], {})
                for e2, J in last.items():
                    if e2 != eng:
                        deps.add(J)
                    elif J is not I:
                        I.odeps.add(J)
        for r in rin + rout:
            if r[1] == "ps":
                self.ps_last[r[0]][eng] = I
        for r in rout:
            lst = self.wr.setdefault(r[0], [])
            lst[:] = [(rr, J) for rr, J in lst if not _contains(r, rr)]
            lst.append((r, I))
            rdd = self.rd.get(r[0])
            if rdd:
                for k in [k for k, (rr, J) in rdd.items() if _contains(r, rr)]:
                    del rdd[k]
        for r in rin:
            self.rd.setdefault(r[0], {})[(I.idx, r[2:])] = (r, I)
        deps.update(extra)
        deps.discard(I)
        I.deps = deps
        self.streams[eng].append(I)
        return I

    def schedule(self, window=400):
        allI = sorted([I for e in self.ENG for I in self.streams[e]], key=lambda I: I.idx)
        n = len(allI)
        placed = [False] * n
        pend = {e: [I for I in allI if I.eng == e] for e in self.ENG}
        ptr = {e: 0 for e in self.ENG}
        free_at = {e: 0.0 for e in self.ENG}
        out = {e: [] for e in self.ENG}
        bl = [0.0] * n
        for I in reversed(allI):
            b = bl[I.idx] + I.cost + I.lat
            bl[I.idx] = b
            for J in I.deps:
                if bl[J.idx] < b:
                    bl[J.idx] = b
            for J in I.odeps:
                if bl[J.idx] < b:
                    bl[J.idx] = b
        first_unplaced = 0
        nplaced = 0
        SLACK = SLACK_NS
        while nplaced < n:
            while first_unplaced < n and placed[first_unplaced]:
                first_unplaced += 1
            lim = first_unplaced + window
            best = None
            for e in self.ENG:
                lst = pend[e]
                while ptr[e] < len(lst) and placed[lst[ptr[e]].idx]:
                    ptr[e] += 1
                k = ptr[e]
                cnt = 0
                e_best = None
                while k < len(lst) and lst[k].idx < lim and cnt < CAND_LIMIT:
                    I = lst[k]
                    k += 1
                    if placed[I.idx]:
                        continue
                    cnt += 1
                    ok = True
                    rdy = 0.0
                    for J in I.deps:
                        if not placed[J.idx]:
                            ok = False
                            break
                        if J.fin > rdy:
                            rdy = J.fin
                    if not ok:
                        continue
                    for J in I.odeps:
                        if not placed[J.idx]:
                            ok = False
                            break
                    if not ok:
                        continue
                    start = max(free_at[e], rdy)
                    if rdy <= free_at[e] + SLACK:
                        key = (0, -bl[I.idx], I.idx)
                    else:
                        key = (1, start, I.idx)
                    if e_best is None or key < e_best[0]:
                        e_best = (key, I, start)
                if e_best is not None:
                    gk = (e_best[2], e_best[1].idx)
                    if best is None or gk < best[0]:
                        best = (gk, e_best[1], e_best[2])
            assert best is not None, "scheduler stuck"
            _, I, start = best
            placed[I.idx] = True
            nplaced += 1
            free_at[I.eng] = start + I.cost
            I.fin = start + I.cost + I.lat
            out[I.eng].append(I)
        self.streams = out
        self.est = max(I.fin for I in allI)
        pos = {}
        for e in self.ENG:
            for k, I in enumerate(out[e]):
                pos[I] = k
        for I in allI:
            latest = {}
            for J in I.deps:
                if J.dma_tag is not None or (J.eng == "pe" and I.eng == "pe"):
                    continue
                if J.eng not in latest or pos[J] > pos[latest[J.eng]]:
                    latest[J.eng] = J
            for J in latest.values():
                J.sig = True
            I.deps = set(J for J in I.deps if J.dma_tag is not None or J in latest.values())

    def emit(self, stack):
        nc = self.nc
        self.eng_sems = {e: stack.enter_context(nc.semaphore("s_" + e)) for e in self.ENG}
        self.dma_sems = {t: stack.enter_context(nc.semaphore("d_" + t)) for t in self.dma_counts}
        for e in self.ENG:
            n = 0
            for I in self.streams[e]:
                if I.sig and I.dma_tag is None:
                    n += 1
                    I.seq = n
        block = stack.enter_context(nc.Block())

        def run(e, eng, final=False):
            waited = {}
            for I in self.streams[e]:
                need = {}
                for J in I.deps:
                    if J.dma_tag is not None:
                        cnt = self.dma_counts[J.dma_tag] if J.dma_tag in self.group_tags else J.dma_cnt
                        key, sem, val = ("d", J.dma_tag), self.dma_sems[J.dma_tag], 16 * cnt
                    else:
                        if J.eng == "pe" and e == "pe":
                            continue
                        key, sem, val = ("e", J.eng), self.eng_sems[J.eng], J.seq
                    if waited.get(key, 0) >= val:
                        continue
                    if key not in need or need[key][1] < val:
                        need[key] = (sem, val)
                for key, (sem, val) in need.items():
                    waited[key] = val
                    eng.wait_ge(sem, val)
                h = I.fn(eng)
                if I.dma_tag is not None:
                    h.then_inc(self.dma_sems[I.dma_tag], 16)
                elif I.sig:
                    h.then_inc(self.eng_sems[e], 1)
            if final:
                for t, c in self.dma_counts.items():
                    eng.wait_ge(self.dma_sems[t], 16 * c)

        @block.tensor
        def _(eng):
            run("pe", eng)

        @block.scalar
        def _(eng):
            run("act", eng)

        @block.vector
        def _(eng):
            run("dve", eng)

        @block.gpsimd
        def _(eng):
            run("pool", eng)

        @block.sync
        def _(eng):
            run("sp", eng, final=True)


def build_nc(NP, NM, dbg=False, window=600):
    NT = NP + NM
    NT1 = NM + 1
    nc = bass.Bass("TRN2", target_bir_lowering=False)
    P = Prog(nc)
    P.group_tags.update(["c0", "c1", "c2", "c3"])
    st = ExitStack()

    def dram(name, shape, dt, kind="ExternalInput"):
        return nc.dram_tensor(name, list(shape), dt, kind=kind).ap()

    x_d = dram("x_all", [NT * 128, D], F32)
    pos_d = dram("pos", [128, NT1], I32)
    wd = {}
    WSPEC = [("rkv", 272), ("qkb", 512), ("vb", 512), ("qa", 512), ("ga", 512), ("gb", 512)]
    for nm, n in WSPEC:
        wd[nm] = dram("w_" + nm, [128, 8, n], F32)
    wout_d = dram("w_out", [128, 8, D], F32)
    wg_d = dram("wg_aug", [128, 256], F32)
    sinks_d = dram("sinks", [8], F32)
    wn_d = dram("wn", [128], F32)
    lng_d = dram("lng", [D], F32)
    lnb_d = dram("lnb", [D], F32)
    invf_d = dram("invf", [16], F32)
    sgn_d = dram("sgn", [16], F32)
    ident_d = dram("ident", [128, 128], F32)
    triinc_d = dram("tri_inc", [128, 128], F32)
    triexc_d = dram("tri_exc", [128, 128], F32)
    gmask_d = dram("gmask", [128, 128], F32)
    mask0_d = dram("mask0", [128, 256], F32)
    mask_d = dram("mask", [128, 256], F32)
    y_d = dram("y", [NM * 128, D], F32, kind="ExternalOutput")

    def sb(name, shape, dt=F32):
        return st.enter_context(nc.sbuf_tensor(name, list(shape), dt))

    def db(name, shape, dt=F32, n=2):
        return [sb("%s_%d" % (name, i), shape, dt) for i in range(n)]

    def ps(name, shape, dt=F32):
        return st.enter_context(nc.psum_tensor(name, list(shape), dt))

    W = {nm: sb("W_" + nm, [128, 8, n], BF16) for nm, n in WSPEC}
    Wout = sb("Wout", [128, 8, D], BF16)
    wg = sb("wg", [128, 256], BF16)
    ident = sb("ident_bf", [128, 128], BF16)
    tri_inc = sb("tri_inc_sb", [128, 128], F32)
    tri_exc = sb("tri_exc_sb", [128, 128], F32)
    gmask = sb("gmask_sb", [128, 128], F32)
    mask0 = sb("mask0_sb", [128, 256], F32)
    maskn = sb("mask_sb", [128, 256], F32)
    sink_bc = sb("sink_bc", [128, 8], F32)
    nsink_bc = sb("nsink_bc", [128, 8], F32)
    wn_bc = sb("wn_bc", [128, 128], F32)
    lng_bc = sb("lng_bc", [128, D], F32)
    lnb_bc = sb("lnb_bc", [128, D], F32)
    invf_bc = sb("invf_bc", [128, 16], F32)
    sgn_bc = sb("sgn_bc", [128, 16], F32)
    eps_t = sb("eps_t", [128, 1], F32)
    one_t = sb("one_t", [128, 1], F32)
    pos_i = sb("pos_i", [128, NT1], I32)
    posf = sb("posf", [128, NT1], F32)
    COS = sb("COS", [128, NT1, 16], F32)
    SINS = sb("SINS", [128, NT1, 16], F32)

    NB4 = 4
    x_bf = db("x_bf", [128, D], BF16, 4)
    x_f32 = db("x_f32", [128, D], F32, 2)
    xT = db("xT", [128, 8, 128], BF16, NB4)
    rb_pad = db("rb_pad", [128, 128], BF16, NB4)
    rbT = db("rbT", [128, 128], BF16, NB4)
    kro = db("kro", [128, 2, 64], BF16)
    kr = db("kr", [128, 2, 256], BF16)
    kT = db("kT", [128, 2, 2, 128], BF16, 3)
    va = db("va", [128, 128], BF16, 3)
    qk_f32 = db("kb_f32", [128, 256], F32, NB4)
    qk_bf = db("qk_bf", [128, 512], BF16)
    e1 = db("e1", [128, 256], F32, NB4)
    sp_ = db("sp", [128, 256], F32, NB4)
    ee2 = db("ee2", [128, 256], F32, NB4)
    eb = db("eb", [128, 2, 128], F32)
    enb = db("enb", [128, 2, 128], F32)
    dec = db("dec", [128, 2, 2], F32, NB4)
    q_decT = db("q_decT", [128, 2, 128], BF16)
    k_invT_lo = db("k_invT_lo", [128, 2, 128], BF16)
    k_invT_hi = db("k_invT_hi", [128, 2, 128], BF16)
    k_end = db("k_end", [128, 256], BF16, NB4)
    attn_bf = db("attn_bf", [128, 4, 128], BF16)
    vb_bf = db("vb_bf", [128, 512], BF16, NB4)
    S_f = [sb("S_f%d" % p, [128, 128], F32) for p in range(2)]
    S_lo = [db("S_lo%d" % p, [128, 128], BF16) for p in range(2)]
    S_hi = [db("S_hi%d" % p, [128, 128], BF16) for p in range(2)]
    junk = sb("junk", [128, 128], F32)
    ss = db("ss", [128, 4], F32)
    lnss = db("lnss", [128, 4], F32)
    rstd_b = db("rstd_b", [128, 4], F32)
    sg = db("sg", [128, 512], F32)
    silu_b = db("silu_b", [128, 512], F32)
    ob1 = db("o1", [128, 512], F32)
    silu_a = db("silu_a", [128, 512], F32)
    mix_bf = db("mix_bf", [128, D], BF16)
    mixT = db("mixT", [128, 8, 128], BF16)
    tmpA = db("tmpA", [128, 8, 16], F32, 2)
    tmpB = db("tmpB", [128, 8, 16], F32, 2)
    qr = db("qr", [128, 8, 64], BF16)
    qT = db("qT", [128, 4, 128], BF16)
    mx = db("mx", [128, 8], F32)
    negm = db("negm", [128, 8], F32)
    sm = db("sm", [128, 2, 256], F32, 3)
    p_bf = db("p_bf", [128, 8, 256], BF16)
    pT = db("pT", [128, 8, 2, 128], BF16)
    rs = db("rs", [128, 8], F32)
    t8 = db("t8", [128, 8], F32)
    es = db("es", [128, 8], F32)
    den = db("den", [128, 8], F32)
    rden = db("rden", [128, 8], F32)
    z_sb = db("z_sb", [128, D], F32)
    NR = NT1 * 16
    assert 3 * NR <= D

    def _view(t_, k, dt=None):
        v = t_[:, k * NR:(k + 1) * NR]
        if dt is not None:
            v = v.bitcast(dt)
        return v.rearrange("p (a b) -> p a b", b=16)

    ang, kq, rr_ = _view(z_sb[0], 0), _view(z_sb[0], 1), _view(z_sb[0], 2)
    rc_, tmpm, kqi = _view(z_sb[1], 0), _view(z_sb[1], 1), _view(z_sb[1], 2, I32)
    stats = db("stats", [128, 2, 6], F32)
    mv = db("mv", [128, 2], F32)
    lnv = db("lnv", [128, 1], F32)
    rstd = db("rstd", [128, 1], F32)
    nmr = db("nmr", [128, 1], F32)

    banks = [ps("bank%d" % i, [128, 512]) for i in range(8)]
    cnt = {"A": 0, "B": 0, "long": 0, "tmp": 0, "sg": 0, "sm": 0}
    NA, NB = RING_A, RING_B

    phase = {"prefix": True}

    def nextP():
        cnt["A"] += 1
        if phase["prefix"]:
            return banks[cnt["A"] % PREFIX_BANKS]
        return banks[cnt["A"] % NA]

    nextS = nextP

    def nextT():
        return nextP()[:].bitcast(BF16)

    def nextBS():
        cnt["B"] += 1
        return banks[NA + cnt["B"] % NB]

    def nextBT():
        return nextBS()[:].bitcast(BF16)

    def nextL():
        cnt["long"] += 1
        return banks[NA + NB + cnt["long"] % (8 - NA - NB)]

    def fsz(ap):
        n = 1
        for s_, c_ in list(ap.ap)[1:]:
            n *= int(c_)
        return n

    def dma(q, out, in_, tag, nbytes=0, after=()):
        return P.add(q, lambda e: e.dma_start(out=out, in_=in_), [out], [in_], dma_tag=tag, cost=60.0,
                     lat=2000.0 + nbytes / 150.0, extra=after)

    def mm(out, lhsT, rhs, start=True, stop=True):
        n = max(64, fsz(rhs))
        c = n / 2.4 * (4.0 if rhs.dtype == F32 else 1.0) + 55.0
        return P.add("pe", lambda e: e.matmul(out, lhsT, rhs, start=start, stop=stop), [out], [lhsT, rhs], cost=c,
                     lat=160.0)

    def tr(out, in_, idn=None):
        idn = ident[:] if idn is None else idn
        P.add("pe", lambda e: e.transpose(out, in_, idn), [out], [in_, idn], cost=110.0, lat=160.0)

    def act(out, in_, func, bias=None, scale=None, accum=None):
        kw = {}
        ins = [in_]
        outs = [out]
        c = 200.0 + 0.84 * fsz(in_)
        if bias is not None:
            kw["bias"] = bias
            if not isinstance(bias, (int, float)):
                ins.append(bias)
                c += 90.0
        if scale is not None:
            kw["scale"] = scale
            if not isinstance(scale, (int, float)):
                ins.append(scale)
        if accum is not None:
            kw["accum_out"] = accum
            outs.append(accum)
            c += 100.0
        P.add("act", lambda e: e.activation(out, in_, func, **kw), outs, ins, cost=c, lat=120.0)

    def ecost(eng, n):
        if eng == "act":
            return 200.0 + 0.84 * n
        if eng == "dve":
            return 150.0 + 1.05 * n
        return 300.0 + 2.2 * n

    def cp(eng, out, in_):
        c = ecost(eng, fsz(in_))
        if eng == "act":
            P.add("act", lambda e: e.copy(out, in_), [out], [in_], cost=c, lat=120.0)
        else:
            P.add(eng, lambda e: e.tensor_copy(out, in_), [out], [in_], cost=c, lat=120.0)

    def tt(eng, out, in0, in1, op):
        P.add(eng, lambda e: e.tensor_tensor(out, in0, in1, op), [out], [in0, in1], cost=ecost(eng, fsz(in0)), lat=120.0)

    def tsc(eng, out, in0, s1, op0, s2=None, op1=None):
        ins = [in0] + [s_ for s_ in (s1, s2) if s_ is not None and not isinstance(s_, (int, float))]
        c = ecost(eng, fsz(in0))
        if op1 is None:
            P.add(eng, lambda e: e.tensor_scalar(out, in0, s1, None, op0), [out], ins, cost=c, lat=120.0)
        else:
            P.add(eng, lambda e: e.tensor_scalar(out, in0, s1, s2, op0, op1), [out], ins, cost=c, lat=120.0)

    def stt(out, in0, scalar, in1, op0, op1):
        ins = [in0, in1] + ([] if isinstance(scalar, (int, float)) else [scalar])
        P.add("dve", lambda e: e.scalar_tensor_tensor(out, in0, scalar, in1, op0, op1), [out], ins,
              cost=210.0 + 1.05 * fsz(in0), lat=120.0)

    def memset(eng, ap, val):
        P.add(eng, lambda e: e.memset(ap, val), [ap], [], cost=ecost(eng, fsz(ap)))

    dma("sp", pos_i[:], pos_d, "c0")
    dma("sp", invf_bc[:], invf_d.partition_broadcast(128), "c0")
    dma("sp", sgn_bc[:], sgn_d.partition_broadcast(128), "c0")
    dma("sp", tri_inc[:], triinc_d, "c2")
    dma("sp", tri_exc[:], triexc_d, "c2")
    dma("sp", gmask[:], gmask_d, "c2")
    dma("sp", mask0[:], mask0_d, "c2")
    dma("sp", maskn[:], mask_d, "c2")
    dma("sp", sink_bc[:], sinks_d.partition_broadcast(128), "c2")
    dma("sp", wn_bc[:], wn_d.partition_broadcast(128), "c2")
    dma("sp", lng_bc[:], lng_d.partition_broadcast(128), "c3", 512 * 1024 * 4)
    dma("sp", lnb_bc[:], lnb_d.partition_broadcast(128), "c3", 512 * 1024 * 4)
    dma("pool", ident[:], ident_d, "c1")
    dma("pool", wg[:], wg_d, "c1")

    issued_x = set()

    def load_x(t):
        if t >= NT or t in issued_x:
            return
        issued_x.add(t)
        dma("pool", x_bf[t % 4][:], x_d[t * 128:(t + 1) * 128, :], "xb%d" % (t % 4), 512 * 1024)

    def load_xf(t):
        if NP <= t < NT:
            dma("sp", x_f32[t % 2][:], x_d[t * 128:(t + 1) * 128, :], "xf%d" % (t % 2), 512 * 1024)

    load_x(0)
    load_xf(NP)
    def wdma(dst, src, tag, ncols, after=()):
        tot = 8 * ncols
        blk = 2048 if tot % 2048 == 0 else tot // 2
        d2 = dst[:].rearrange("p c n -> p (c n)").rearrange("p (a b) -> p a b", b=blk)
        s2 = src.rearrange("p c n -> p (c n)").rearrange("p (a b) -> p a b", b=blk)
        return dma("pool", d2, s2, tag, 128 * 4 * tot, after=after)

    def wsmall(nm, after):
        return dma("pool", W[nm][:], wd[nm], "w_" + nm, 8 * 128 * 4 * dict(WSPEC)[nm], after=after)

    wprev = [wsmall("rkv", ())]
    load_x(1)
    load_x(2)
    load_x(3)
    for nm in ("qkb", "vb"):
        wprev = [wsmall(nm, wprev)]
    late_list = [("gb", W["gb"], wd["gb"], 512), ("qa", W["qa"], wd["qa"], 512), ("ga", W["ga"], wd["ga"], 512),
                 ("out", Wout, wout_d, D)]

    def issue_late_weights(n=1):
        for _ in range(n):
            if late_list:
                nm, dst, src, ncols = late_list.pop(0)
                wdma(dst, src, "w_" + nm, ncols, after=wprev)

    if NP < 8:
        issue_late_weights(4)

    memset("dve", eps_t[:], EPS)
    memset("dve", one_t[:], 1.0)
    for i in range(NB4):
        memset("pool", rb_pad[i][:], 0.0)
        memset("pool", rb_pad[i][:, 16:17], 1.0)
    for i in range(2):
        memset("pool", kr[i][:], 0.0)
        memset("dve", S_f[i][:], 0.0)
        memset("pool", k_invT_lo[i][:], 0.0)
        memset("pool", k_invT_hi[i][:], 0.0)
        for p in range(2):
            memset("pool", S_lo[p][i][:], 0.0)
            memset("pool", S_hi[p][i][:], 0.0)
    for i in range(3):
        memset("pool", kT[i][:], 0.0)
        memset("pool", va[i][:], 0.0)
    tsc("dve", nsink_bc[:], sink_bc[:], -1.0, ALU.mult)

    cp("dve", posf[:], pos_i[:])
    tt("dve", ang[:], posf[:].unsqueeze(2).broadcast_to([128, NT1, 16]),
       invf_bc[:].unsqueeze(1).broadcast_to([128, NT1, 16]), ALU.mult)

    def range_reduce(dst, src):
        tsc("dve", kq[:], src, 1.0 / TWO_PI, ALU.mult)
        cp("dve", kqi[:], kq[:])
        cp("dve", kq[:], kqi[:])
        stt(dst, kq[:], -CW1, src, ALU.mult, ALU.add)
        stt(dst, kq[:], -CW2, dst, ALU.mult, ALU.add)
        stt(dst, kq[:], -CW3, dst, ALU.mult, ALU.add)
        tsc("dve", tmpm[:], dst, -math.pi, ALU.is_lt, TWO_PI, ALU.mult)
        tt("dve", dst, dst, tmpm[:], ALU.add)
        tsc("dve", tmpm[:], dst, math.pi, ALU.is_gt, -TWO_PI, ALU.mult)
        tt("dve", dst, dst, tmpm[:], ALU.add)

    ang_src = z_sb[0][:, 0:NR].bitcast(I32)
    ang_ri = silu_a[0][:, 0:NR].bitcast(I32)
    P.add("dve", lambda e: e.tensor_scalar(ang_ri, ang_src, 0, None, ALU.bitwise_or), [ang_ri], [ang_src], cost=300.0)
    ang_r = silu_a[0][:, 0:NR].rearrange("p (a b) -> p a b", b=16)
    range_reduce(rr_[:], ang_r)
    tsc("dve", rc_[:], rr_[:], math.pi / 2.0, ALU.add)
    tsc("dve", tmpm[:], rc_[:], math.pi, ALU.is_gt, -TWO_PI, ALU.mult)
    tt("dve", rc_[:], rc_[:], tmpm[:], ALU.add)
    tsc("dve", rr_[:], rr_[:], PI_SAFE, ALU.min, -PI_SAFE, ALU.max)
    tsc("dve", rc_[:], rc_[:], PI_SAFE, ALU.min, -PI_SAFE, ALU.max)
    act(SINS[:], rr_[:], AF.Sin)
    act(COS[:], rc_[:], AF.Sin)
    tt("dve", SINS[:], SINS[:], sgn_bc[:].unsqueeze(1).broadcast_to([128, NT1, 16]), ALU.mult)

    def proj(bank, xT_, wname, lo, n):
        for c in range(8):
            mm(bank[:, 0:n], xT_[:, c, :], W[wname][:, c, lo:lo + n], start=(c == 0), stop=(c == 7))

    def proj_pair(xT_, A, B):
        (bankA, wA, nA), (bankB, wB, nB) = A, B
        prev = None
        for c in range(8):
            a = mm(bankA[:, 0:nA], xT_[:, c, :], W[wA][:, c, 0:nA], start=(c == 0), stop=(c == 7))
            if prev is not None:
                a.odeps.add(prev)
            b_ = mm(bankB[:, 0:nB], xT_[:, c, :], W[wB][:, c, 0:nB], start=(c == 0), stop=(c == 7))
            b_.odeps.add(a)
            prev = b_

    def rope(psv, nh, dst, ti):
        cnt["tmp"] += 1
        tA = tmpA[cnt["tmp"] % 2]
        tB = tmpB[cnt["tmp"] % 2]
        cosb = COS[:, ti:ti + 1, :].broadcast_to([128, nh, 16])
        tt("dve", tA[:, 0:nh, :], psv[:, :, 0:16], cosb, ALU.mult)
        tt("dve", tB[:, 0:nh, 0:8], psv[:, :, 8:16], SINS[:, ti:ti + 1, 0:8].broadcast_to([128, nh, 8]), ALU.mult)
        tt("dve", tB[:, 0:nh, 8:16], psv[:, :, 0:8], SINS[:, ti:ti + 1, 8:16].broadcast_to([128, nh, 8]), ALU.mult)
        cp("act", dst[:, :, 16:64], psv[:, :, 16:64])
        tt("pool", dst[:, :, 0:16], tA[:, 0:nh, :], tB[:, 0:nh, :], ALU.add)

    def silu_from(bank, dst):
        cnt["sg"] += 1
        sg_ = sg[cnt["sg"] % 2]
        act(sg_[:], bank[:, 0:512], AF.Exp, scale=-1.0)
        cp("act", dst, bank[:, 0:512])
        act(sg_[:], sg_[:], AF.Ln, bias=one_t[:, 0:1])
        act(sg_[:], sg_[:], AF.Exp, scale=-1.0)
        tt("pool", dst, dst, sg_[:], ALU.mult)

    def front(t):
        main = t >= NP
        halo = (t == NP - 1)
        m = t - NP
        s = t % 2
        s3 = t % 3
        p3 = (t - 1) % 3
        ti = m + 1
        load_x(t + 3)
        if t >= 3:
            issue_late_weights()
        s4 = t % NB4
        xT_ = xT[s4]
        phase["prefix"] = (not main) and PREFIX_WIDE
        Pb = nextP()
        T = Pb[:].bitcast(BF16)
        for c in range(8):
            tr(T[:, c * 128:(c + 1) * 128], x_bf[t % 4][:, c * 128:(c + 1) * 128])
        cp(XT_ENG, xT_[:].rearrange("p c t -> p (c t)"), T[:])

        if main or halo:
            proj(Pb, xT_, "rkv", 0, 272)
        else:
            proj(Pb, xT_, "rkv", 0, 16)
        cp("dve", rb_pad[s4][:, 0:16], Pb[:, 0:16])
        if main or halo:
            rope(Pb[:, 16:144].rearrange("p (h d) -> p h d", h=2), 2, kro[s][:], ti if main else 0)
            cp("pool", kr[s][:, :, 0:64], kro[s][:])
            cp("pool", kr[s][:, :, 192:256], kro[s][:])
            cp("act", va[s3][:], Pb[:, 144:272])
            T = Pb[:].bitcast(BF16)
            for g in range(2):
                for v in range(2):
                    tr(T[:, (g * 2 + v) * 128:(g * 2 + v + 1) * 128], kr[s][:, g, v * 128:(v + 1) * 128])
            cp("dve", kT[s3][:].rearrange("p g v t -> p (g v t)"), T[:, 0:512])
        yield
        if main or halo:
            Pb = nextP()
        if main:
            proj(Pb, xT_, "qkb", 0, 512)
        else:
            for c in range(8):
                mm(Pb[:, 256:512], xT_[:, c, :], W["qkb"][:, c, 256:512], start=(c == 0), stop=(c == 7))
        if main:
            cp("act", qk_bf[s][:], Pb[:])
        cp("act", qk_f32[s4][:], Pb[:, 256:512])
        Sg = nextS() if (main or halo) else Pb
        T = Sg[:].bitcast(BF16)
        tr(T[:, 0:128], rb_pad[s4][:])
        cp("dve", rbT[s4][:], T[:, 0:128])
        mm(Sg[:, 0:256], rbT[s4][:], wg[:])
        act(e1[s4][:], Sg[:, 0:256], AF.Exp, scale=-1.0)
        act(sp_[s4][:], e1[s4][:], AF.Ln, bias=one_t[:, 0:1])
        Sb = Sg
        mm(Sb[:, 0:256], tri_exc[:], sp_[s4][:])
        if main:
            for p in range(2):
                mm(Sb[:, 256 + p * 128:256 + (p + 1) * 128], sp_[s4][:, p * 128:(p + 1) * 128], tri_inc[:])
            act(ee2[s4][:], Sb[:, 0:256], AF.Exp)
            bT = Sb[:, 256:512].rearrange("p (a t) -> p a t", a=2)
            act(eb[s][:], bT, AF.Exp)
            act(enb[s][:], bT, AF.Exp, scale=-1.0)
            dcol = [eb[s][:, p, 127:128] for p in range(2)]
        else:
            for p in range(2):
                mm(Sb[:, 256 + p * 2:256 + (p + 1) * 2], sp_[s4][:, p * 128:(p + 1) * 128], tri_inc[:, 126:128])
            act(ee2[s4][:], Sb[:, 0:256], AF.Exp)
            act(dec[s4][:], Sb[:, 256:260].rearrange("p (a t) -> p a t", a=2), AF.Exp)
            dcol = [dec[s4][:, p, 1:2] for p in range(2)]
        tt("pool" if main else "dve", k_end[s4][:], qk_f32[s4][:], ee2[s4][:], ALU.mult)
        yield
        Pv = nextP()
        proj(Pv, xT_, "vb", 0, 512)
        cp("act", vb_bf[s4][:], Pv[:])

        if main:
            Sa = nextS()
            T = Sa[:].bitcast(BF16)
            for i in range(4):
                tr(T[:, i * 128:(i + 1) * 128], qk_bf[s][:, i * 128:(i + 1) * 128])
            stt(q_decT[s][:], T[:, 0:256].rearrange("p (a t) -> p a t", a=2), SCALE_B, eb[s][:], ALU.mult, ALU.mult)
            tt("dve", k_invT_lo[s][0:64, :, :], T[0:64, 256:512].rearrange("p (a t) -> p a t", a=2),
               enb[s][0:64, :, :], ALU.mult)
            tt("dve", k_invT_hi[s][64:128, :, :], T[64:128, 256:512].rearrange("p (a t) -> p a t", a=2),
               enb[s][64:128, :, :], ALU.mult)
            for h in range(4):
                p, r = h // 2, h % 2
                kk = k_invT_lo[s] if r == 0 else k_invT_hi[s]
                mm(Sa[:, h * 128:(h + 1) * 128], kk[:, p, :], q_decT[s][:, p, :])
            tt("dve", attn_bf[s][:], Sa[:].rearrange("p (h t) -> p h t", h=4),
               gmask[:].unsqueeze(1).broadcast_to([128, 4, 128]), ALU.mult)
            ps_O = nextL()
            for h in range(4):
                p, r = h // 2, h % 2
                Sx = S_lo[p][s] if r == 0 else S_hi[p][s]
                mm(ps_O[:, h * 128:(h + 1) * 128], attn_bf[s][:, h, :], vb_bf[s4][:, h * 128:(h + 1) * 128],
                   start=True, stop=False)
                mm(ps_O[:, h * 128:(h + 1) * 128], q_decT[s][:, p, :], Sx[:], start=False, stop=True)

        yield
        Su = Pv
        for p in range(2):
            mm(Su[:, p * 256:(p + 1) * 256], k_end[s4][:, p * 128:(p + 1) * 128], vb_bf[s4][:, p * 256:(p + 1) * 256])
        for p in range(2):
            for r in range(2):
                rows = slice(r * 64, (r + 1) * 64)
                stt(S_f[p][rows, :], S_f[p][rows, :], dcol[p][rows, :],
                    Su[rows, p * 256 + r * 128:p * 256 + (r + 1) * 128], ALU.mult, ALU.add)
            if main or halo:
                cp("pool", S_lo[p][1 - s][0:64, :], S_f[p][0:64, :])
                cp("pool", S_hi[p][1 - s][64:128, :], S_f[p][64:128, :])
        if not main:
            return

        for h in range(4):
            act(junk[:], ps_O[:, h * 128:(h + 1) * 128], AF.Square, accum=ss[s][:, h:h + 1])
        tt("dve", ob1[s][:].rearrange("p (h d) -> p h d", h=4), ps_O[:].rearrange("p (h d) -> p h d", h=4),
           wn_bc[:].unsqueeze(1).broadcast_to([128, 4, 128]), ALU.mult)
        act(lnss[s][:], ss[s][:], AF.Ln, bias=eps_t[:, 0:1], scale=1.0 / 128.0)
        act(rstd_b[s][:], lnss[s][:], AF.Exp, scale=-0.5)
        Pb = nextP()
        proj(Pb, xT_, "gb", 0, 512)
        silu_from(Pb, silu_b[s][:])
        for h in range(4):
            stt(mix_bf[s][:, 512 + h * 128:512 + (h + 1) * 128], ob1[s][:, h * 128:(h + 1) * 128],
                rstd_b[s][:, h:h + 1], silu_b[s][:, h * 128:(h + 1) * 128], ALU.mult, ALU.mult)

    def back(t):
        m = t - NP
        s = t % 2
        s3 = t % 3
        p3 = (t - 1) % 3
        ti = m + 1
        xT_ = xT[t % NB4]
        Pb = nextP()
        Pga = nextP()
        proj_pair(xT_, (Pb, "qa", 512), (Pga, "ga", 512))
        rope(Pb[:].rearrange("p (h d) -> p h d", h=8), 8, qr[s][:], ti)
        silu_from(Pga, silu_a[s][:])
        yield
        T = nextBT()
        for j in range(4):
            tr(T[:, j * 128:(j + 1) * 128], qr[s][:, 2 * j:2 * j + 2, :].rearrange("p h d -> p (h d)"))
        cp("act", qT[s][:].rearrange("p j t -> p (j t)"), T[:, 0:512])
        msk = mask0 if m == 0 else maskn
        for j in range(4):
            g = j // 2
            bank = nextBS()
            for r in range(2):
                mm(bank[:, r * 256:r * 256 + 128], qT[s][:, j, :], kT[p3][:, g, r, :])
                mm(bank[:, r * 256 + 128:(r + 1) * 256], qT[s][:, j, :], kT[s3][:, g, r, :])
            bv = bank[:].rearrange("p (r k) -> p r k", r=2)
            mxs = mx[s][:, 2 * j:2 * j + 2]
            ngs = negm[s][:, 2 * j:2 * j + 2]
            cnt["sm"] += 1
            smj = sm[cnt["sm"] % 3]
            P.add("dve", (lambda e, o=mxs, i=bv: e.tensor_reduce(o, i, AX.X, ALU.max)), [mxs], [bv],
                  cost=ecost("dve", 512), lat=120.0)
            tt("dve", smj[:], bv, msk[:].unsqueeze(1).broadcast_to([128, 2, 256]), ALU.add)
            tsc("dve", ngs, mxs, -SCALE_A, ALU.mult)
            tt("dve", ngs, ngs, nsink_bc[:, 2 * j:2 * j + 2], ALU.min)
            for r in range(2):
                h = 2 * j + r
                act(p_bf[s][:, h, :], smj[:, r, :], AF.Exp, bias=negm[s][:, h:h + 1], scale=SCALE_A,
                    accum=rs[s][:, h:h + 1])
            T = nextBT()
            for r in range(2):
                h = 2 * j + r
                for blk in range(2):
                    tr(T[:, (r * 2 + blk) * 128:(r * 2 + blk + 1) * 128], p_bf[s][:, h, blk * 128:(blk + 1) * 128])
            cp("dve" if j % 2 == 0 else "act", pT[s][:, 2 * j:2 * j + 2, :, :].rearrange("p h b t -> p (h b t)"),
               T[:, 0:512])
            if j == 1:
                yield
        ps_O = nextL()
        for h in range(8):
            g = h // 4
            mm(ps_O[:, h * 64:(h + 1) * 64], pT[s][:, h, 0, :], va[p3][:, g * 64:(g + 1) * 64], start=True, stop=False)
            mm(ps_O[:, h * 64:(h + 1) * 64], pT[s][:, h, 1, :], va[s3][:, g * 64:(g + 1) * 64], start=False, stop=True)
        tt("dve", t8[s][:], negm[s][:], sink_bc[:], ALU.add)
        act(es[s][:], t8[s][:], AF.Exp)
        tt("dve", den[s][:], rs[s][:], es[s][:], ALU.add)
        P.add("dve", (lambda e, o=rden[s][:], i=den[s][:]: e.reciprocal(o, i)), [rden[s][:]], [den[s][:]], cost=150.0,
              lat=120.0)
        for h in range(8):
            stt(mix_bf[s][:, h * 64:(h + 1) * 64], ps_O[:, h * 64:(h + 1) * 64], rden[s][:, h:h + 1],
                silu_a[s][:, h * 64:(h + 1) * 64], ALU.mult, ALU.mult)

        yield
        T = nextBT()
        for c in range(8):
            tr(T[:, c * 128:(c + 1) * 128], mix_bf[s][:, c * 128:(c + 1) * 128])
        cp("act", mixT[s][:].rearrange("p c t -> p (c t)"), T[:])
        ps_Y = [nextL(), nextL()] if Y_LONG else [nextP(), nextP()]
        prev = None
        for c in range(8):
            for hf in range(2):
                y_ = mm(ps_Y[hf][:], mixT[s][:, c, :], Wout[:, c, hf * 512:(hf + 1) * 512], start=(c == 0),
                        stop=(c == 7))
                if prev is not None:
                    y_.odeps.add(prev)
                prev = y_
        for hf in range(2):
            zs = z_sb[s][:, hf * 512:(hf + 1) * 512]
            stt(zs, x_f32[s][:, hf * 512:(hf + 1) * 512], ALPHA, ps_Y[hf][:], ALU.mult, ALU.add)
            P.add("dve", (lambda e, o=stats[s][:, hf, :], i=zs: e.bn_stats(o, i)), [stats[s][:, hf, :]], [zs],
                  cost=ecost("dve", 512), lat=120.0)
        P.add("dve", (lambda e, o=mv[s][:], i=stats[s][:].rearrange("p a b -> p (a b)"): e.bn_aggr(o, i)),
              [mv[s][:]], [stats[s][:]], cost=120.0, lat=120.0)
        act(lnv[s][:], mv[s][:, 1:2], AF.Ln, bias=eps_t[:, 0:1])
        act(rstd[s][:], lnv[s][:], AF.Exp, scale=-0.5)
        tsc("dve", nmr[s][:], mv[s][:, 0:1], rstd[s][:, 0:1], ALU.mult, -1.0, ALU.mult)
        act(z_sb[s][:], z_sb[s][:], AF.Identity, bias=nmr[s][:, 0:1], scale=rstd[s][:, 0:1])
        tt("pool", z_sb[s][:], z_sb[s][:], lng_bc[:], ALU.mult)
        tt("dve", z_sb[s][:], z_sb[s][:], lnb_bc[:], ALU.add)
        dma("sp", y_d[m * 128:(m + 1) * 128, :], z_sb[s][:], "y%d" % s, 512 * 1024)
        load_xf(t + 2)

    def drain(g):
        for _ in g:
            pass

    load_xf(NP + 1)
    for t in range(min(NP + 1, NT)):
        drain(front(t))
    for t in range(NP, NT):
        gb = back(t)
        gf = front(t + 1) if t + 1 < NT else iter(())
        done_b = done_f = False
        while not (done_b and done_f):
            if not done_b:
                try:
                    next(gb)
                except StopIteration:
                    done_b = True
            if not done_f:
                try:
                    next(gf)
                except StopIteration:
                    done_f = True

    P.schedule(window)
    P.emit(st)
    st.close()
    nc._sched_est = P.est
    return nc


_QA, _KA, _VA, _GA, _QB, _KB, _VB, _GB, _RB = 0, 512, 640, 768, 1280, 1536, 1792, 2304, 2816


def _host_consts():
    j = np.arange(128)[:, None]
    i = np.arange(128)[None, :]
    c = {}
    c["ident"] = np.eye(128, dtype=np.float32)
    c["tri_inc"] = np.where(j <= i, -1.0 / 16.0, 0.0).astype(np.float32)
    c["tri_exc"] = np.where(j > i, -1.0 / 16.0, 0.0).astype(np.float32)
    c["gmask"] = (j <= i).astype(np.float32)
    q = np.arange(128)[:, None]
    k = np.arange(128)[None, :]
    prev = np.where(k > q, 0.0, NEG)
    cur = np.where(k <= q, 0.0, NEG)
    c["mask"] = np.concatenate([prev, cur], axis=1).astype(np.float32)
    c["mask_first"] = np.concatenate([np.full((128, 128), NEG), cur], axis=1).astype(np.float32)
    half = 8
    invf = (np.float32(500000.0) ** (-np.arange(half, dtype=np.float32) / np.float32(half))).astype(np.float32)
    c["invf"] = np.concatenate([invf, invf]).astype(np.float32)
    c["sgn"] = np.concatenate([-np.ones(8), np.ones(8)]).astype(np.float32)
    return c


def _weight_maps(w_in, gla_w_gate_up, gla_b_gate, attn_sinks, gla_norm_w, w_out, ln_g, ln_b):
    w = np.asarray(w_in, np.float32)[0]

    def chunks(cols):
        a = np.ascontiguousarray(cols)
        return np.ascontiguousarray(a.reshape(8, 128, a.shape[1]).transpose(1, 0, 2))

    m = {}
    m["w_rkv"] = chunks(np.concatenate([w[:, _RB:_RB + 16], w[:, _KA:_KA + 128], w[:, _VA:_VA + 128]], axis=1))
    m["w_qkb"] = chunks(np.concatenate([w[:, _QB:_QB + 256], w[:, _KB:_KB + 256]], axis=1))
    m["w_vb"] = chunks(w[:, _VB:_VB + 512])
    m["w_qa"] = chunks(w[:, _QA:_QA + 512])
    m["w_ga"] = chunks(w[:, _GA:_GA + 512])
    m["w_gb"] = chunks(w[:, _GB:_GB + 512])
    m["w_out"] = chunks(np.asarray(w_out, np.float32)[0])
    wg = np.zeros((128, 256), np.float32)
    wg[0:16] = np.asarray(gla_w_gate_up, np.float32)[0]
    wg[16] = np.asarray(gla_b_gate, np.float32)[0]
    m["wg_aug"] = wg
    m["sinks"] = np.ascontiguousarray(np.asarray(attn_sinks, np.float32)[0])
    m["wn"] = np.ascontiguousarray(np.asarray(gla_norm_w, np.float32)[0])
    m["lng"] = np.ascontiguousarray(np.asarray(ln_g, np.float32)[0])
    m["lnb"] = np.ascontiguousarray(np.asarray(ln_b, np.float32)[0])
    return m


def make_in_maps(x, positions, wmaps, NP, NM, assign):
    x = np.asarray(x, np.float32)
    positions = np.asarray(positions, np.int32)
    seg = NM * 128
    consts = _host_consts()
    maps = []
    for (b, start) in assign:
        pre = NP * 128
        xa = np.zeros((pre + seg, D), np.float32)
        lo = max(0, start - pre)
        xa[pre - (start - lo):pre] = x[b, lo:start]
        xa[pre:] = x[b, start:start + seg]
        pos = np.zeros((NM + 1) * 128, np.int32)
        if start >= 128:
            pos[0:128] = positions[b, start - 128:start]
        pos[128:] = positions[b, start:start + seg]
        m = dict(wmaps)
        m["x_all"] = xa
        m["pos"] = np.ascontiguousarray(pos.reshape(NM + 1, 128).T)
        for k in ("ident", "tri_inc", "tri_exc", "gmask", "mask", "invf", "sgn"):
            m[k] = consts[k]
        m["mask0"] = consts["mask"] if (start > 0 and NP > 0) else consts["mask_first"]
        maps.append(m)
    return maps


_NC_CACHE = {}


def kernel(x, positions, w_in, gla_w_gate_up, gla_b_gate, attn_sinks, gla_norm_w, w_out, ln_g, ln_b):
    NM = HALF // 128
    NP = HALF // 128
    key = (NP, NM)
    if key not in _NC_CACHE:
        _NC_CACHE[key] = build_nc(NP, NM)
    nc = _NC_CACHE[key]
    wmaps = _weight_maps(w_in, gla_w_gate_up, gla_b_gate, attn_sinks, gla_norm_w, w_out, ln_g, ln_b)
    assign = [(c // 2, (c % 2) * HALF) for c in range(NCORES)]
    maps = make_in_maps(x, positions, wmaps, NP, NM, assign)
    res = run_bass_kernel_spmd(nc, maps, core_ids=list(range(NCORES)))
    out = np.empty((BATCH, SEQ, D), np.float32)
    for c in range(NCORES):
        b, sg_ = c // 2, c % 2
        out[b, sg_ * HALF:(sg_ + 1) * HALF] = np.asarray(res.results[c]["y"], np.float32)
    return out
```

```python
import math
from contextlib import ExitStack

import numpy as np
import concourse.bass as bass
import concourse.mybir as mybir
from concourse.bass_utils import run_bass_kernel_spmd

F32 = mybir.dt.float32
BF16 = mybir.dt.bfloat16
I32 = mybir.dt.int32
AF = mybir.ActivationFunctionType
ALU = mybir.AluOpType
AX = mybir.AxisListType

D = 1024
NCORES = 8
SEQ = 4096
BATCH = 4
HALF = SEQ // 2
EPS = 1e-5
ALPHA = 2.0 ** 0.25
SCALE_A = 64 ** -0.5
SCALE_B = 64 ** -0.5
NEG = -30000.0
CAND_LIMIT = 130
RING_A, RING_B = 3, 2
Y_LONG = True
PREFIX_WIDE = True
PREFIX_BANKS = 8
SLACK_NS = 0.0
XT_ENG = "dve"
TWO_PI = 2.0 * math.pi
PI_SAFE = 3.1415925
CW1 = 6.28125
CW2 = float(np.float32(TWO_PI - CW1))
CW3 = float(TWO_PI - CW1 - float(np.float32(TWO_PI - CW1)))


class Ins:
    __slots__ = ("eng", "fn", "deps", "sig", "seq", "dma_tag", "dma_cnt", "idx", "cost", "odeps", "fin", "lat")

    def __init__(self, eng, fn, dma_tag):
        self.eng = eng
        self.fn = fn
        self.idx = 0
        self.cost = 100.0
        self.lat = 0.0
        self.odeps = set()
        self.fin = 0.0
        self.deps = set()
        self.sig = False
        self.seq = 0
        self.dma_tag = dma_tag
        self.dma_cnt = 0


def _region(ap):
    t = ap.tensor
    name = t.name
    esz = mybir.dt.size(ap.dtype)
    dims = [(int(s) * esz, int(c)) for s, c in ap.ap]
    off = int(ap.offset) * esz
    space = str(getattr(t, "space", ""))
    kind = "sb"
    cls = type(t).__name__
    if "DRam" in cls or "Dram" in cls or "DRAM" in cls:
        kind = "dram"
    elif "PSum" in cls or "Psum" in cls or "PSUM" in cls:
        kind = "ps"
    if kind == "dram":
        ext = sum((c - 1) * abs(s) for s, c in dims)
        return (name, kind, 0, 1, off, off + ext + esz)
    pstep = dims[0][0]
    if pstep <= 0:
        return (name, kind, 0, 128, 0, 1 << 40)
    p0 = off // pstep
    f0 = off % pstep
    ext = sum((c - 1) * abs(s) for s, c in dims[1:])
    return (name, kind, p0, p0 + dims[0][1], f0, f0 + ext + esz)


def _ovl(a, b):
    return a[2] < b[3] and b[2] < a[3] and a[4] < b[5] and b[4] < a[5]


def _contains(a, b):
    return a[2] <= b[2] and b[3] <= a[3] and a[4] <= b[4] and b[5] <= a[5]


class Prog:
    ENG = ["pe", "act", "dve", "pool", "sp"]

    def __init__(self, nc):
        self.nc = nc
        self.streams = {e: [] for e in self.ENG}
        self.wr = {}
        self.rd = {}
        self.ps_last = {}
        self.dma_counts = {}
        self.group_tags = set()
        self.n = 0
        self.last_dma = {}

    def add(self, eng, fn, outs, ins, dma_tag=None, cost=100.0, lat=0.0, extra=()):
        I = Ins(eng, fn, dma_tag)
        I.idx = self.n
        self.n += 1
        I.cost = cost
        I.lat = lat
        if dma_tag is not None:
            prevq = self.last_dma.get(eng)
            if prevq is not None:
                I.odeps.add(prevq)
            self.last_dma[eng] = I
        if dma_tag is not None:
            self.dma_counts[dma_tag] = self.dma_counts.get(dma_tag, 0) + 1
            I.dma_cnt = self.dma_counts[dma_tag]
        rin = [_region(a) for a in ins]
        rout = [_region(a) for a in outs]
        deps = set()
        for r in rin:
            for rr, J in self.wr.get(r[0], ()):
                if _ovl(r, rr):
                    deps.add(J)
        for r in rout:
            for rr, J in self.wr.get(r[0], ()):
                if _ovl(r, rr):
                    deps.add(J)
            for rr, J in self.rd.get(r[0], {}).values():
                if _ovl(r, rr):
                    deps.add(J)
        for r in rin + rout:
            if r[1] == "ps":
                last = self.ps_last.setdefault(r[0], {})
                for e2, J in last.items():
                    if e2 != eng:
                        deps.add(J)
                    elif J is not I:
                        I.odeps.add(J)
        for r in rin + rout:
            if r[1] == "ps":
                self.ps_last[r[0]][eng] = I
        for r in rout:
            lst = self.wr.setdefault(r[0], [])
            lst[:] = [(rr, J) for rr, J in lst if not _contains(r, rr)]
            lst.append((r, I))
            rdd = self.rd.get(r[0])
            if rdd:
                for k in [k for k, (rr, J) in rdd.items() if _contains(r, rr)]:
                    del rdd[k]
        for r in rin:
            self.rd.setdefault(r[0], {})[(I.idx, r[2:])] = (r, I)
        deps.update(extra)
        deps.discard(I)
        I.deps = deps
        self.streams[eng].append(I)
        return I

    def schedule(self, window=400):
        allI = sorted([I for e in self.ENG for I in self.streams[e]], key=lambda I: I.idx)
        n = len(allI)
        placed = [False] * n
        pend = {e: [I for I in allI if I.eng == e] for e in self.ENG}
        ptr = {e: 0 for e in self.ENG}
        free_at = {e: 0.0 for e in self.ENG}
        out = {e: [] for e in self.ENG}
        bl = [0.0] * n
        for I in reversed(allI):
            b = bl[I.idx] + I.cost + I.lat
            bl[I.idx] = b
            for J in I.deps:
                if bl[J.idx] < b:
                    bl[J.idx] = b
            for J in I.odeps:
                if bl[J.idx] < b:
                    bl[J.idx] = b
        first_unplaced = 0
        nplaced = 0
        SLACK = SLACK_NS
        while nplaced < n:
            while first_unplaced < n and placed[first_unplaced]:
                first_unplaced += 1
            lim = first_unplaced + window
            best = None
            for e in self.ENG:
                lst = pend[e]
                while ptr[e] < len(lst) and placed[lst[ptr[e]].idx]:
                    ptr[e] += 1
                k = ptr[e]
                cnt = 0
                e_best = None
                while k < len(lst) and lst[k].idx < lim and cnt < CAND_LIMIT:
                    I = lst[k]
                    k += 1
                    if placed[I.idx]:
                        continue
                    cnt += 1
                    ok = True
                    rdy = 0.0
                    for J in I.deps:
                        if not placed[J.idx]:
                            ok = False
                            break
                        if J.fin > rdy:
                            rdy = J.fin
                    if not ok:
                        continue
                    for J in I.odeps:
                        if not placed[J.idx]:
                            ok = False
                            break
                    if not ok:
                        continue
                    start = max(free_at[e], rdy)
                    if rdy <= free_at[e] + SLACK:
                        key = (0, -bl[I.idx], I.idx)
                    else:
                        key = (1, start, I.idx)
                    if e_best is None or key < e_best[0]:
                        e_best = (key, I, start)
                if e_best is not None:
                    gk = (e_best[2], e_best[1].idx)
                    if best is None or gk < best[0]:
                        best = (gk, e_best[1], e_best[2])
            assert best is not None, "scheduler stuck"
            _, I, start = best
            placed[I.idx] = True
            nplaced += 1
            free_at[I.eng] = start + I.cost
            I.fin = start + I.cost + I.lat
            out[I.eng].append(I)
        self.streams = out
        self.est = max(I.fin for I in allI)
        pos = {}
        for e in self.ENG:
            for k, I in enumerate(out[e]):
                pos[I] = k
        for I in allI:
            latest = {}
            for J in I.deps:
                if J.dma_tag is not None or (J.eng == "pe" and I.eng == "pe"):
                    continue
                if J.eng not in latest or pos[J] > pos[latest[J.eng]]:
                    latest[J.eng] = J
            for J in latest.values():
                J.sig = True
            I.deps = set(J for J in I.deps if J.dma_tag is not None or J in latest.values())

    def emit(self, stack):
        nc = self.nc
        self.eng_sems = {e: stack.enter_context(nc.semaphore("s_" + e)) for e in self.ENG}
        self.dma_sems = {t: stack.enter_context(nc.semaphore("d_" + t)) for t in self.dma_counts}
        for e in self.ENG:
            n = 0
            for I in self.streams[e]:
                if I.sig and I.dma_tag is None:
                    n += 1
                    I.seq = n
        block = stack.enter_context(nc.Block())

        def run(e, eng, final=False):
            waited = {}
            for I in self.streams[e]:
                need = {}
                for J in I.deps:
                    if J.dma_tag is not None:
                        cnt = self.dma_counts[J.dma_tag] if J.dma_tag in self.group_tags else J.dma_cnt
                        key, sem, val = ("d", J.dma_tag), self.dma_sems[J.dma_tag], 16 * cnt
                    else:
                        if J.eng == "pe" and e == "pe":
                            continue
                        key, sem, val = ("e", J.eng), self.eng_sems[J.eng], J.seq
                    if waited.get(key, 0) >= val:
                        continue
                    if key not in need or need[key][1] < val:
                        need[key] = (sem, val)
                for key, (sem, val) in need.items():
                    waited[key] = val
                    eng.wait_ge(sem, val)
                h = I.fn(eng)
                if I.dma_tag is not None:
                    h.then_inc(self.dma_sems[I.dma_tag], 16)
                elif I.sig:
                    h.then_inc(self.eng_sems[e], 1)
            if final:
                for t, c in self.dma_counts.items():
                    eng.wait_ge(self.dma_sems[t], 16 * c)

        @block.tensor
        def _(eng):
            run("pe", eng)

        @block.scalar
        def _(eng):
            run("act", eng)

        @block.vector
        def _(eng):
            run("dve", eng)

        @block.gpsimd
        def _(eng):
            run("pool", eng)

        @block.sync
        def _(eng):
            run("sp", eng, final=True)


def build_nc(NP, NM, dbg=False, window=600):
    NT = NP + NM
    NT1 = NM + 1
    nc = bass.Bass("TRN2", target_bir_lowering=False)
    P = Prog(nc)
    P.group_tags.update(["c0", "c1", "c2", "c3"])
    st = ExitStack()

    def dram(name, shape, dt, kind="ExternalInput"):
        return nc.dram_tensor(name, list(shape), dt, kind=kind).ap()

    x_d = dram("x_all", [NT * 128, D], F32)
    pos_d = dram("pos", [128, NT1], I32)
    wd = {}
    WSPEC = [("rkv", 272), ("qkb", 512), ("vb", 512), ("qa", 512), ("ga", 512), ("gb", 512)]
    for nm, n in WSPEC:
        wd[nm] = dram("w_" + nm, [128, 8, n], F32)
    wout_d = dram("w_out", [128, 8, D], F32)
    wg_d = dram("wg_aug", [128, 256], F32)
    sinks_d = dram("sinks", [8], F32)
    wn_d = dram("wn", [128], F32)
    lng_d = dram("lng", [D], F32)
    lnb_d = dram("lnb", [D], F32)
    invf_d = dram("invf", [16], F32)
    sgn_d = dram("sgn", [16], F32)
    ident_d = dram("ident", [128, 128], F32)
    triinc_d = dram("tri_inc", [128, 128], F32)
    triexc_d = dram("tri_exc", [128, 128], F32)
    gmask_d = dram("gmask", [128, 128], F32)
    mask0_d = dram("mask0", [128, 256], F32)
    mask_d = dram("mask", [128, 256], F32)
    y_d = dram("y", [NM * 128, D], F32, kind="ExternalOutput")

    def sb(name, shape, dt=F32):
        return st.enter_context(nc.sbuf_tensor(name, list(shape), dt))

    def db(name, shape, dt=F32, n=2):
        return [sb("%s_%d" % (name, i), shape, dt) for i in range(n)]

    def ps(name, shape, dt=F32):
        return st.enter_context(nc.psum_tensor(name, list(shape), dt))

    W = {nm: sb("W_" + nm, [128, 8, n], BF16) for nm, n in WSPEC}
    Wout = sb("Wout", [128, 8, D], BF16)
    wg = sb("wg", [128, 256], BF16)
    ident = sb("ident_bf", [128, 128], BF16)
    tri_inc = sb("tri_inc_sb", [128, 128], F32)
    tri_exc = sb("tri_exc_sb", [128, 128], F32)
    gmask = sb("gmask_sb", [128, 128], F32)
    mask0 = sb("mask0_sb", [128, 256], F32)
    maskn = sb("mask_sb", [128, 256], F32)
    sink_bc = sb("sink_bc", [128, 8], F32)
    nsink_bc = sb("nsink_bc", [128, 8], F32)
    wn_bc = sb("wn_bc", [128, 128], F32)
    lng_bc = sb("lng_bc", [128, D], F32)
    lnb_bc = sb("lnb_bc", [128, D], F32)
    invf_bc = sb("invf_bc", [128, 16], F32)
    sgn_bc = sb("sgn_bc", [128, 16], F32)
    eps_t = sb("eps_t", [128, 1], F32)
    one_t = sb("one_t", [128, 1], F32)
    pos_i = sb("pos_i", [128, NT1], I32)
    posf = sb("posf", [128, NT1], F32)
    COS = sb("COS", [128, NT1, 16], F32)
    SINS = sb("SINS", [128, NT1, 16], F32)

    NB4 = 4
    x_bf = db("x_bf", [128, D], BF16, 4)
    x_f32 = db("x_f32", [128, D], F32, 2)
    xT = db("xT", [128, 8, 128], BF16, NB4)
    rb_pad = db("rb_pad", [128, 128], BF16, NB4)
    rbT = db("rbT", [128, 128], BF16, NB4)
    kro = db("kro", [128, 2, 64], BF16)
    kr = db("kr", [128, 2, 256], BF16)
    kT = db("kT", [128, 2, 2, 128], BF16, 3)
    va = db("va", [128, 128], BF16, 3)
    qk_f32 = db("kb_f32", [128, 256], F32, NB4)
    qk_bf = db("qk_bf", [128, 512], BF16)
    e1 = db("e1", [128, 256], F32, NB4)
    sp_ = db("sp", [128, 256], F32, NB4)
    ee2 = db("ee2", [128, 256], F32, NB4)
    eb = db("eb", [128, 2, 128], F32)
    enb = db("enb", [128, 2, 128], F32)
    dec = db("dec", [128, 2, 2], F32, NB4)
    q_decT = db("q_decT", [128, 2, 128], BF16)
    k_invT_lo = db("k_invT_lo", [128, 2, 128], BF16)
    k_invT_hi = db("k_invT_hi", [128, 2, 128], BF16)
    k_end = db("k_end", [128, 256], BF16, NB4)
    attn_bf = db("attn_bf", [128, 4, 128], BF16)
    vb_bf = db("vb_bf", [128, 512], BF16, NB4)
    S_f = [sb("S_f%d" % p, [128, 128], F32) for p in range(2)]
    S_lo = [db("S_lo%d" % p, [128, 128], BF16) for p in range(2)]
    S_hi = [db("S_hi%d" % p, [128, 128], BF16) for p in range(2)]
    junk = sb("junk", [128, 128], F32)
    ss = db("ss", [128, 4], F32)
    lnss = db("lnss", [128, 4], F32)
    rstd_b = db("rstd_b", [128, 4], F32)
    sg = db("sg", [128, 512], F32)
    silu_b = db("silu_b", [128, 512], F32)
    ob1 = db("o1", [128, 512], F32)
    silu_a = db("silu_a", [128, 512], F32)
    mix_bf = db("mix_bf", [128, D], BF16)
    mixT = db("mixT", [128, 8, 128], BF16)
    tmpA = db("tmpA", [128, 8, 16], F32, 2)
    tmpB = db("tmpB", [128, 8, 16], F32, 2)
    qr = db("qr", [128, 8, 64], BF16)
    qT = db("qT", [128, 4, 128], BF16)
    mx = db("mx", [128, 8], F32)
    negm = db("negm", [128, 8], F32)
    sm = db("sm", [128, 2, 256], F32, 3)
    p_bf = db("p_bf", [128, 8, 256], BF16)
    pT = db("pT", [128, 8, 2, 128], BF16)
    rs = db("rs", [128, 8], F32)
    t8 = db("t8", [128, 8], F32)
    es = db("es", [128, 8], F32)
    den = db("den", [128, 8], F32)
    rden = db("rden", [128, 8], F32)
    z_sb = db("z_sb", [128, D], F32)
    NR = NT1 * 16
    assert 3 * NR <= D

    def _view(t_, k, dt=None):
        v = t_[:, k * NR:(k + 1) * NR]
        if dt is not None:
            v = v.bitcast(dt)
        return v.rearrange("p (a b) -> p a b", b=16)

    ang, kq, rr_ = _view(z_sb[0], 0), _view(z_sb[0], 1), _view(z_sb[0], 2)
    rc_, tmpm, kqi = _view(z_sb[1], 0), _view(z_sb[1], 1), _view(z_sb[1], 2, I32)
    stats = db("stats", [128, 2, 6], F32)
    mv = db("mv", [128, 2], F32)
    lnv = db("lnv", [128, 1], F32)
    rstd = db("rstd", [128, 1], F32)
    nmr = db("nmr", [128, 1], F32)

    banks = [ps("bank%d" % i, [128, 512]) for i in range(8)]
    cnt = {"A": 0, "B": 0, "long": 0, "tmp": 0, "sg": 0, "sm": 0}
    NA, NB = RING_A, RING_B

    phase = {"prefix": True}

    def nextP():
        cnt["A"] += 1
        if phase["prefix"]:
            return banks[cnt["A"] % PREFIX_BANKS]
        return banks[cnt["A"] % NA]

    nextS = nextP

    def nextT():
        return nextP()[:].bitcast(BF16)

    def nextBS():
        cnt["B"] += 1
        return banks[NA + cnt["B"] % NB]

    def nextBT():
        return nextBS()[:].bitcast(BF16)

    def nextL():
        cnt["long"] += 1
        return banks[NA + NB + cnt["long"] % (8 - NA - NB)]

    def fsz(ap):
        n = 1
        for s_, c_ in list(ap.ap)[1:]:
            n *= int(c_)
        return n

    def dma(q, out, in_, tag, nbytes=0, after=()):
        return P.add(q, lambda e: e.dma_start(out=out, in_=in_), [out], [in_], dma_tag=tag, cost=60.0,
                     lat=2000.0 + nbytes / 150.0, extra=after)

    def mm(out, lhsT, rhs, start=True, stop=True):
        n = max(64, fsz(rhs))
        c = n / 2.4 * (4.0 if rhs.dtype == F32 else 1.0) + 40.0
        return P.add("pe", lambda e: e.matmul(out, lhsT, rhs, start=start, stop=stop), [out], [lhsT, rhs], cost=c,
                     lat=60.0)

    def tr(out, in_, idn=None):
        idn = ident[:] if idn is None else idn
        P.add("pe", lambda e: e.transpose(out, in_, idn), [out], [in_, idn], cost=95.0, lat=60.0)

    def act(out, in_, func, bias=None, scale=None, accum=None):
        kw = {}
        ins = [in_]
        outs = [out]
        c = 200.0 + 0.84 * fsz(in_)
        if bias is not None:
            kw["bias"] = bias
            if not isinstance(bias, (int, float)):
                ins.append(bias)
                c += 90.0
        if scale is not None:
            kw["scale"] = scale
            if not isinstance(scale, (int, float)):
                ins.append(scale)
        if accum is not None:
            kw["accum_out"] = accum
            outs.append(accum)
            c += 100.0
        P.add("act", lambda e: e.activation(out, in_, func, **kw), outs, ins, cost=c, lat=120.0)

    def ecost(eng, n):
        if eng == "act":
            return 200.0 + 0.84 * n
        if eng == "dve":
            return 150.0 + 1.05 * n
        return 300.0 + 2.2 * n

    def cp(eng, out, in_):
        c = ecost(eng, fsz(in_))
        if eng == "act":
            P.add("act", lambda e: e.copy(out, in_), [out], [in_], cost=c, lat=120.0)
        else:
            P.add(eng, lambda e: e.tensor_copy(out, in_), [out], [in_], cost=c, lat=120.0)

    def tt(eng, out, in0, in1, op):
        P.add(eng, lambda e: e.tensor_tensor(out, in0, in1, op), [out], [in0, in1], cost=ecost(eng, fsz(in0)), lat=120.0)

    def tsc(eng, out, in0, s1, op0, s2=None, op1=None):
        ins = [in0] + [s_ for s_ in (s1, s2) if s_ is not None and not isinstance(s_, (int, float))]
        c = ecost(eng, fsz(in0))
        if op1 is None:
            P.add(eng, lambda e: e.tensor_scalar(out, in0, s1, None, op0), [out], ins, cost=c, lat=120.0)
        else:
            P.add(eng, lambda e: e.tensor_scalar(out, in0, s1, s2, op0, op1), [out], ins, cost=c, lat=120.0)

    def stt(out, in0, scalar, in1, op0, op1):
        ins = [in0, in1] + ([] if isinstance(scalar, (int, float)) else [scalar])
        P.add("dve", lambda e: e.scalar_tensor_tensor(out, in0, scalar, in1, op0, op1), [out], ins,
              cost=210.0 + 1.05 * fsz(in0), lat=120.0)

    def memset(eng, ap, val):
        P.add(eng, lambda e: e.memset(ap, val), [ap], [], cost=ecost(eng, fsz(ap)))

    dma("sp", pos_i[:], pos_d, "c0")
    dma("sp", invf_bc[:], invf_d.partition_broadcast(128), "c0")
    dma("sp", sgn_bc[:], sgn_d.partition_broadcast(128), "c0")
    dma("sp", tri_inc[:], triinc_d, "c2")
    dma("sp", tri_exc[:], triexc_d, "c2")
    dma("sp", gmask[:], gmask_d, "c2")
    dma("sp", mask0[:], mask0_d, "c2")
    dma("sp", maskn[:], mask_d, "c2")
    dma("sp", sink_bc[:], sinks_d.partition_broadcast(128), "c2")
    dma("sp", wn_bc[:], wn_d.partition_broadcast(128), "c2")
    dma("sp", lng_bc[:], lng_d.partition_broadcast(128), "c3", 512 * 1024 * 4)
    dma("sp", lnb_bc[:], lnb_d.partition_broadcast(128), "c3", 512 * 1024 * 4)
    dma("pool", ident[:], ident_d, "c1")
    dma("pool", wg[:], wg_d, "c1")

    issued_x = set()

    def load_x(t):
        if t >= NT or t in issued_x:
            return
        issued_x.add(t)
        dma("pool", x_bf[t % 4][:], x_d[t * 128:(t + 1) * 128, :], "xb%d" % (t % 4), 512 * 1024)

    def load_xf(t):
        if NP <= t < NT:
            dma("sp", x_f32[t % 2][:], x_d[t * 128:(t + 1) * 128, :], "xf%d" % (t % 2), 512 * 1024)

    load_x(0)
    load_xf(NP)
    def wdma(dst, src, tag, ncols, after=()):
        tot = 8 * ncols
        blk = 2048 if tot % 2048 == 0 else tot // 2
        d2 = dst[:].rearrange("p c n -> p (c n)").rearrange("p (a b) -> p a b", b=blk)
        s2 = src.rearrange("p c n -> p (c n)").rearrange("p (a b) -> p a b", b=blk)
        return dma("pool", d2, s2, tag, 128 * 4 * tot, after=after)

    def wsmall(nm, after):
        return dma("pool", W[nm][:], wd[nm], "w_" + nm, 8 * 128 * 4 * dict(WSPEC)[nm], after=after)

    wprev = [wsmall("rkv", ())]
    load_x(1)
    load_x(2)
    load_x(3)
    for nm in ("qkb", "vb"):
        wprev = [wsmall(nm, wprev)]
    late_list = [("gb", W["gb"], wd["gb"], 512), ("qa", W["qa"], wd["qa"], 512), ("ga", W["ga"], wd["ga"], 512),
                 ("out", Wout, wout_d, D)]

    def issue_late_weights(n=1):
        for _ in range(n):
            if late_list:
                nm, dst, src, ncols = late_list.pop(0)
                wdma(dst, src, "w_" + nm, ncols, after=wprev)

    if NP < 8:
        issue_late_weights(4)

    memset("dve", eps_t[:], EPS)
    memset("dve", one_t[:], 1.0)
    for i in range(NB4):
        memset("pool", rb_pad[i][:], 0.0)
        memset("pool", rb_pad[i][:, 16:17], 1.0)
    for i in range(2):
        memset("pool", kr[i][:], 0.0)
        memset("dve", S_f[i][:], 0.0)
        memset("pool", k_invT_lo[i][:], 0.0)
        memset("pool", k_invT_hi[i][:], 0.0)
        for p in range(2):
            memset("pool", S_lo[p][i][:], 0.0)
            memset("pool", S_hi[p][i][:], 0.0)
    for i in range(3):
        memset("pool", kT[i][:], 0.0)
        memset("pool", va[i][:], 0.0)
    tsc("dve", nsink_bc[:], sink_bc[:], -1.0, ALU.mult)

    cp("dve", posf[:], pos_i[:])
    tt("dve", ang[:], posf[:].unsqueeze(2).broadcast_to([128, NT1, 16]),
       invf_bc[:].unsqueeze(1).broadcast_to([128, NT1, 16]), ALU.mult)

    def range_reduce(dst, src):
        tsc("dve", kq[:], src, 1.0 / TWO_PI, ALU.mult)
        cp("dve", kqi[:], kq[:])
        cp("dve", kq[:], kqi[:])
        stt(dst, kq[:], -CW1, src, ALU.mult, ALU.add)
        stt(dst, kq[:], -CW2, dst, ALU.mult, ALU.add)
        stt(dst, kq[:], -CW3, dst, ALU.mult, ALU.add)
        tsc("dve", tmpm[:], dst, -math.pi, ALU.is_lt, TWO_PI, ALU.mult)
        tt("dve", dst, dst, tmpm[:], ALU.add)
        tsc("dve", tmpm[:], dst, math.pi, ALU.is_gt, -TWO_PI, ALU.mult)
        tt("dve", dst, dst, tmpm[:], ALU.add)

    ang_src = z_sb[0][:, 0:NR].bitcast(I32)
    ang_ri = silu_a[0][:, 0:NR].bitcast(I32)
    P.add("dve", lambda e: e.tensor_scalar(ang_ri, ang_src, 0, None, ALU.bitwise_or), [ang_ri], [ang_src], cost=300.0)
    ang_r = silu_a[0][:, 0:NR].rearrange("p (a b) -> p a b", b=16)
    range_reduce(rr_[:], ang_r)
    tsc("dve", rc_[:], rr_[:], math.pi / 2.0, ALU.add)
    tsc("dve", tmpm[:], rc_[:], math.pi, ALU.is_gt, -TWO_PI, ALU.mult)
    tt("dve", rc_[:], rc_[:], tmpm[:], ALU.add)
    tsc("dve", rr_[:], rr_[:], PI_SAFE, ALU.min, -PI_SAFE, ALU.max)
    tsc("dve", rc_[:], rc_[:], PI_SAFE, ALU.min, -PI_SAFE, ALU.max)
    act(SINS[:], rr_[:], AF.Sin)
    act(COS[:], rc_[:], AF.Sin)
    tt("dve", SINS[:], SINS[:], sgn_bc[:].unsqueeze(1).broadcast_to([128, NT1, 16]), ALU.mult)

    def proj(bank, xT_, wname, lo, n):
        for c in range(8):
            mm(bank[:, 0:n], xT_[:, c, :], W[wname][:, c, lo:lo + n], start=(c == 0), stop=(c == 7))

    def proj_pair(xT_, A, B):
        (bankA, wA, nA), (bankB, wB, nB) = A, B
        prev = None
        for c in range(8):
            a = mm(bankA[:, 0:nA], xT_[:, c, :], W[wA][:, c, 0:nA], start=(c == 0), stop=(c == 7))
            if prev is not None:
                a.odeps.add(prev)
            b_ = mm(bankB[:, 0:nB], xT_[:, c, :], W[wB][:, c, 0:nB], start=(c == 0), stop=(c == 7))
            b_.odeps.add(a)
            prev = b_

    def rope(psv, nh, dst, ti):
        cnt["tmp"] += 1
        tA = tmpA[cnt["tmp"] % 2]
        tB = tmpB[cnt["tmp"] % 2]
        cosb = COS[:, ti:ti + 1, :].broadcast_to([128, nh, 16])
        tt("dve", tA[:, 0:nh, :], psv[:, :, 0:16], cosb, ALU.mult)
        tt("dve", tB[:, 0:nh, 0:8], psv[:, :, 8:16], SINS[:, ti:ti + 1, 0:8].broadcast_to([128, nh, 8]), ALU.mult)
        tt("dve", tB[:, 0:nh, 8:16], psv[:, :, 0:8], SINS[:, ti:ti + 1, 8:16].broadcast_to([128, nh, 8]), ALU.mult)
        cp("act", dst[:, :, 16:64], psv[:, :, 16:64])
        tt("pool", dst[:, :, 0:16], tA[:, 0:nh, :], tB[:, 0:nh, :], ALU.add)

    def silu_from(bank, dst):
        cnt["sg"] += 1
        sg_ = sg[cnt["sg"] % 2]
        act(sg_[:], bank[:, 0:512], AF.Exp, scale=-1.0)
        cp("act", dst, bank[:, 0:512])
        act(sg_[:], sg_[:], AF.Ln, bias=one_t[:, 0:1])
        act(sg_[:], sg_[:], AF.Exp, scale=-1.0)
        tt("pool", dst, dst, sg_[:], ALU.mult)

    def front(t):
        main = t >= NP
        halo = (t == NP - 1)
        m = t - NP
        s = t % 2
        s3 = t % 3
        p3 = (t - 1) % 3
        ti = m + 1
        load_x(t + 3)
        if t >= 3:
            issue_late_weights()
        s4 = t % NB4
        xT_ = xT[s4]
        phase["prefix"] = (not main) and PREFIX_WIDE
        Pb = nextP()
        T = Pb[:].bitcast(BF16)
        for c in range(8):
            tr(T[:, c * 128:(c + 1) * 128], x_bf[t % 4][:, c * 128:(c + 1) * 128])
        cp(XT_ENG, xT_[:].rearrange("p c t -> p (c t)"), T[:])

        if main or halo:
            proj(Pb, xT_, "rkv", 0, 272)
        else:
            proj(Pb, xT_, "rkv", 0, 16)
        cp("dve", rb_pad[s4][:, 0:16], Pb[:, 0:16])
        if main or halo:
            rope(Pb[:, 16:144].rearrange("p (h d) -> p h d", h=2), 2, kro[s][:], ti if main else 0)
            cp("pool", kr[s][:, :, 0:64], kro[s][:])
            cp("pool", kr[s][:, :, 192:256], kro[s][:])
            cp("act", va[s3][:], Pb[:, 144:272])
            T = Pb[:].bitcast(BF16)
            for g in range(2):
                for v in range(2):
                    tr(T[:, (g * 2 + v) * 128:(g * 2 + v + 1) * 128], kr[s][:, g, v * 128:(v + 1) * 128])
            cp("dve", kT[s3][:].rearrange("p g v t -> p (g v t)"), T[:, 0:512])
        yield
        if main or halo:
            Pb = nextP()
        if main:
            proj(Pb, xT_, "qkb", 0, 512)
        else:
            for c in range(8):
                mm(Pb[:, 256:512], xT_[:, c, :], W["qkb"][:, c, 256:512], start=(c == 0), stop=(c == 7))
        if main:
            cp("act", qk_bf[s][:], Pb[:])
        cp("act", qk_f32[s4][:], Pb[:, 256:512])
        Sg = nextS() if (main or halo) else Pb
        T = Sg[:].bitcast(BF16)
        tr(T[:, 0:128], rb_pad[s4][:])
        cp("dve", rbT[s4][:], T[:, 0:128])
        mm(Sg[:, 0:256], rbT[s4][:], wg[:])
        act(e1[s4][:], Sg[:, 0:256], AF.Exp, scale=-1.0)
        act(sp_[s4][:], e1[s4][:], AF.Ln, bias=one_t[:, 0:1])
        Sb = Sg
        mm(Sb[:, 0:256], tri_exc[:], sp_[s4][:])
        if main:
            for p in range(2):
                mm(Sb[:, 256 + p * 128:256 + (p + 1) * 128], sp_[s4][:, p * 128:(p + 1) * 128], tri_inc[:])
            act(ee2[s4][:], Sb[:, 0:256], AF.Exp)
            bT = Sb[:, 256:512].rearrange("p (a t) -> p a t", a=2)
            act(eb[s][:], bT, AF.Exp)
            act(enb[s][:], bT, AF.Exp, scale=-1.0)
            dcol = [eb[s][:, p, 127:128] for p in range(2)]
        else:
            for p in range(2):
                mm(Sb[:, 256 + p * 2:256 + (p + 1) * 2], sp_[s4][:, p * 128:(p + 1) * 128], tri_inc[:, 126:128])
            act(ee2[s4][:], Sb[:, 0:256], AF.Exp)
            act(dec[s4][:], Sb[:, 256:260].rearrange("p (a t) -> p a t", a=2), AF.Exp)
            dcol = [dec[s4][:, p, 1:2] for p in range(2)]
        tt("pool" if main else "dve", k_end[s4][:], qk_f32[s4][:], ee2[s4][:], ALU.mult)
        yield
        Pv = nextP()
        proj(Pv, xT_, "vb", 0, 512)
        cp("act", vb_bf[s4][:], Pv[:])

        if main:
            Sa = nextS()
            T = Sa[:].bitcast(BF16)
            for i in range(4):
                tr(T[:, i * 128:(i + 1) * 128], qk_bf[s][:, i * 128:(i + 1) * 128])
            stt(q_decT[s][:], T[:, 0:256].rearrange("p (a t) -> p a t", a=2), SCALE_B, eb[s][:], ALU.mult, ALU.mult)
            tt("dve", k_invT_lo[s][0:64, :, :], T[0:64, 256:512].rearrange("p (a t) -> p a t", a=2),
               enb[s][0:64, :, :], ALU.mult)
            tt("dve", k_invT_hi[s][64:128, :, :], T[64:128, 256:512].rearrange("p (a t) -> p a t", a=2),
               enb[s][64:128, :, :], ALU.mult)
            for h in range(4):
                p, r = h // 2, h % 2
                kk = k_invT_lo[s] if r == 0 else k_invT_hi[s]
                mm(Sa[:, h * 128:(h + 1) * 128], kk[:, p, :], q_decT[s][:, p, :])
            tt("dve", attn_bf[s][:], Sa[:].rearrange("p (h t) -> p h t", h=4),
               gmask[:].unsqueeze(1).broadcast_to([128, 4, 128]), ALU.mult)
            ps_O = nextL()
            for h in range(4):
                p, r = h // 2, h % 2
                Sx = S_lo[p][s] if r == 0 else S_hi[p][s]
                mm(ps_O[:, h * 128:(h + 1) * 128], attn_bf[s][:, h, :], vb_bf[s4][:, h * 128:(h + 1) * 128],
                   start=True, stop=False)
                mm(ps_O[:, h * 128:(h + 1) * 128], q_decT[s][:, p, :], Sx[:], start=False, stop=True)

        yield
        Su = Pv
        for p in range(2):
            mm(Su[:, p * 256:(p + 1) * 256], k_end[s4][:, p * 128:(p + 1) * 128], vb_bf[s4][:, p * 256:(p + 1) * 256])
        for p in range(2):
            for r in range(2):
                rows = slice(r * 64, (r + 1) * 64)
                stt(S_f[p][rows, :], S_f[p][rows, :], dcol[p][rows, :],
                    Su[rows, p * 256 + r * 128:p * 256 + (r + 1) * 128], ALU.mult, ALU.add)
            if main or halo:
                cp("pool", S_lo[p][1 - s][0:64, :], S_f[p][0:64, :])
                cp("pool", S_hi[p][1 - s][64:128, :], S_f[p][64:128, :])
        if not main:
            return

        for h in range(4):
            act(junk[:], ps_O[:, h * 128:(h + 1) * 128], AF.Square, accum=ss[s][:, h:h + 1])
        tt("dve", ob1[s][:].rearrange("p (h d) -> p h d", h=4), ps_O[:].rearrange("p (h d) -> p h d", h=4),
           wn_bc[:].unsqueeze(1).broadcast_to([128, 4, 128]), ALU.mult)
        act(lnss[s][:], ss[s][:], AF.Ln, bias=eps_t[:, 0:1], scale=1.0 / 128.0)
        act(rstd_b[s][:], lnss[s][:], AF.Exp, scale=-0.5)
        Pb = nextP()
        proj(Pb, xT_, "gb", 0, 512)
        silu_from(Pb, silu_b[s][:])
        for h in range(4):
            stt(mix_bf[s][:, 512 + h * 128:512 + (h + 1) * 128], ob1[s][:, h * 128:(h + 1) * 128],
                rstd_b[s][:, h:h + 1], silu_b[s][:, h * 128:(h + 1) * 128], ALU.mult, ALU.mult)

    def back(t):
        m = t - NP
        s = t % 2
        s3 = t % 3
        p3 = (t - 1) % 3
        ti = m + 1
        xT_ = xT[t % NB4]
        Pb = nextP()
        Pga = nextP()
        proj_pair(xT_, (Pb, "qa", 512), (Pga, "ga", 512))
        rope(Pb[:].rearrange("p (h d) -> p h d", h=8), 8, qr[s][:], ti)
        silu_from(Pga, silu_a[s][:])
        yield
        T = nextBT()
        for j in range(4):
            tr(T[:, j * 128:(j + 1) * 128], qr[s][:, 2 * j:2 * j + 2, :].rearrange("p h d -> p (h d)"))
        cp("act", qT[s][:].rearrange("p j t -> p (j t)"), T[:, 0:512])
        msk = mask0 if m == 0 else maskn
        for j in range(4):
            g = j // 2
            bank = nextBS()
            for r in range(2):
                mm(bank[:, r * 256:r * 256 + 128], qT[s][:, j, :], kT[p3][:, g, r, :])
                mm(bank[:, r * 256 + 128:(r + 1) * 256], qT[s][:, j, :], kT[s3][:, g, r, :])
            bv = bank[:].rearrange("p (r k) -> p r k", r=2)
            mxs = mx[s][:, 2 * j:2 * j + 2]
            ngs = negm[s][:, 2 * j:2 * j + 2]
            cnt["sm"] += 1
            smj = sm[cnt["sm"] % 3]
            P.add("dve", (lambda e, o=mxs, i=bv: e.tensor_reduce(o, i, AX.X, ALU.max)), [mxs], [bv],
                  cost=ecost("dve", 512), lat=120.0)
            tt("dve", smj[:], bv, msk[:].unsqueeze(1).broadcast_to([128, 2, 256]), ALU.add)
            tsc("dve", ngs, mxs, -SCALE_A, ALU.mult)
            tt("dve", ngs, ngs, nsink_bc[:, 2 * j:2 * j + 2], ALU.min)
            for r in range(2):
                h = 2 * j + r
                act(p_bf[s][:, h, :], smj[:, r, :], AF.Exp, bias=negm[s][:, h:h + 1], scale=SCALE_A,
                    accum=rs[s][:, h:h + 1])
            T = nextBT()
            for r in range(2):
                h = 2 * j + r
                for blk in range(2):
                    tr(T[:, (r * 2 + blk) * 128:(r * 2 + blk + 1) * 128], p_bf[s][:, h, blk * 128:(blk + 1) * 128])
            cp("dve" if j % 2 == 0 else "act", pT[s][:, 2 * j:2 * j + 2, :, :].rearrange("p h b t -> p (h b t)"),
               T[:, 0:512])
            if j == 1:
                yield
        ps_O = nextL()
        for h in range(8):
            g = h // 4
            mm(ps_O[:, h * 64:(h + 1) * 64], pT[s][:, h, 0, :], va[p3][:, g * 64:(g + 1) * 64], start=True, stop=False)
            mm(ps_O[:, h * 64:(h + 1) * 64], pT[s][:, h, 1, :], va[s3][:, g * 64:(g + 1) * 64], start=False, stop=True)
        tt("dve", t8[s][:], negm[s][:], sink_bc[:], ALU.add)
        act(es[s][:], t8[s][:], AF.Exp)
        tt("dve", den[s][:], rs[s][:], es[s][:], ALU.add)
        P.add("dve", (lambda e, o=rden[s][:], i=den[s][:]: e.reciprocal(o, i)), [rden[s][:]], [den[s][:]], cost=150.0,
              lat=120.0)
        for h in range(8):
            stt(mix_bf[s][:, h * 64:(h + 1) * 64], ps_O[:, h * 64:(h + 1) * 64], rden[s][:, h:h + 1],
                silu_a[s][:, h * 64:(h + 1) * 64], ALU.mult, ALU.mult)

        yield
        T = nextBT()
        for c in range(8):
            tr(T[:, c * 128:(c + 1) * 128], mix_bf[s][:, c * 128:(c + 1) * 128])
        cp("act", mixT[s][:].rearrange("p c t -> p (c t)"), T[:])
        ps_Y = [nextL(), nextL()] if Y_LONG else [nextP(), nextP()]
        prev = None
        for c in range(8):
            for hf in range(2):
                y_ = mm(ps_Y[hf][:], mixT[s][:, c, :], Wout[:, c, hf * 512:(hf + 1) * 512], start=(c == 0),
                        stop=(c == 7))
                if prev is not None:
                    y_.odeps.add(prev)
                prev = y_
        for hf in range(2):
            zs = z_sb[s][:, hf * 512:(hf + 1) * 512]
            stt(zs, x_f32[s][:, hf * 512:(hf + 1) * 512], ALPHA, ps_Y[hf][:], ALU.mult, ALU.add)
            P.add("dve", (lambda e, o=stats[s][:, hf, :], i=zs: e.bn_stats(o, i)), [stats[s][:, hf, :]], [zs],
                  cost=ecost("dve", 512), lat=120.0)
        P.add("dve", (lambda e, o=mv[s][:], i=stats[s][:].rearrange("p a b -> p (a b)"): e.bn_aggr(o, i)),
              [mv[s][:]], [stats[s][:]], cost=120.0, lat=120.0)
        act(lnv[s][:], mv[s][:, 1:2], AF.Ln, bias=eps_t[:, 0:1])
        act(rstd[s][:], lnv[s][:], AF.Exp, scale=-0.5)
        tsc("dve", nmr[s][:], mv[s][:, 0:1], rstd[s][:, 0:1], ALU.mult, -1.0, ALU.mult)
        act(z_sb[s][:], z_sb[s][:], AF.Identity, bias=nmr[s][:, 0:1], scale=rstd[s][:, 0:1])
        tt("pool", z_sb[s][:], z_sb[s][:], lng_bc[:], ALU.mult)
        tt("dve", z_sb[s][:], z_sb[s][:], lnb_bc[:], ALU.add)
        dma("sp", y_d[m * 128:(m + 1) * 128, :], z_sb[s][:], "y%d" % s, 512 * 1024)
        load_xf(t + 2)

    def drain(g):
        for _ in g:
            pass

    load_xf(NP + 1)
    for t in range(min(NP + 1, NT)):
        drain(front(t))
    for t in range(NP, NT):
        gb = back(t)
        gf = front(t + 1) if t + 1 < NT else iter(())
        done_b = done_f = False
        while not (done_b and done_f):
            if not done_b:
                try:
                    next(gb)
                except StopIteration:
                    done_b = True
            if not done_f:
                try:
                    next(gf)
                except StopIteration:
                    done_f = True

    P.schedule(window)
    P.emit(st)
    st.close()
    nc._sched_est = P.est
    return nc


_QA, _KA, _VA, _GA, _QB, _KB, _VB, _GB, _RB = 0, 512, 640, 768, 1280, 1536, 1792, 2304, 2816


def _host_consts():
    j = np.arange(128)[:, None]
    i = np.arange(128)[None, :]
    c = {}
    c["ident"] = np.eye(128, dtype=np.float32)
    c["tri_inc"] = np.where(j <= i, -1.0 / 16.0, 0.0).astype(np.float32)
    c["tri_exc"] = np.where(j > i, -1.0 / 16.0, 0.0).astype(np.float32)
    c["gmask"] = (j <= i).astype(np.float32)
    q = np.arange(128)[:, None]
    k = np.arange(128)[None, :]
    prev = np.where(k > q, 0.0, NEG)
    cur = np.where(k <= q, 0.0, NEG)
    c["mask"] = np.concatenate([prev, cur], axis=1).astype(np.float32)
    c["mask_first"] = np.concatenate([np.full((128, 128), NEG), cur], axis=1).astype(np.float32)
    half = 8
    invf = (np.float32(500000.0) ** (-np.arange(half, dtype=np.float32) / np.float32(half))).astype(np.float32)
    c["invf"] = np.concatenate([invf, invf]).astype(np.float32)
    c["sgn"] = np.concatenate([-np.ones(8), np.ones(8)]).astype(np.float32)
    return c


def _weight_maps(w_in, gla_w_gate_up, gla_b_gate, attn_sinks, gla_norm_w, w_out, ln_g, ln_b):
    w = np.asarray(w_in, np.float32)[0]

    def chunks(cols):
        a = np.ascontiguousarray(cols)
        return np.ascontiguousarray(a.reshape(8, 128, a.shape[1]).transpose(1, 0, 2))

    m = {}
    m["w_rkv"] = chunks(np.concatenate([w[:, _RB:_RB + 16], w[:, _KA:_KA + 128], w[:, _VA:_VA + 128]], axis=1))
    m["w_qkb"] = chunks(np.concatenate([w[:, _QB:_QB + 256], w[:, _KB:_KB + 256]], axis=1))
    m["w_vb"] = chunks(w[:, _VB:_VB + 512])
    m["w_qa"] = chunks(w[:, _QA:_QA + 512])
    m["w_ga"] = chunks(w[:, _GA:_GA + 512])
    m["w_gb"] = chunks(w[:, _GB:_GB + 512])
    m["w_out"] = chunks(np.asarray(w_out, np.float32)[0])
    wg = np.zeros((128, 256), np.float32)
    wg[0:16] = np.asarray(gla_w_gate_up, np.float32)[0]
    wg[16] = np.asarray(gla_b_gate, np.float32)[0]
    m["wg_aug"] = wg
    m["sinks"] = np.ascontiguousarray(np.asarray(attn_sinks, np.float32)[0])
    m["wn"] = np.ascontiguousarray(np.asarray(gla_norm_w, np.float32)[0])
    m["lng"] = np.ascontiguousarray(np.asarray(ln_g, np.float32)[0])
    m["lnb"] = np.ascontiguousarray(np.asarray(ln_b, np.float32)[0])
    return m


def make_in_maps(x, positions, wmaps, NP, NM, assign):
    x = np.asarray(x, np.float32)
    positions = np.asarray(positions, np.int32)
    seg = NM * 128
    consts = _host_consts()
    maps = []
    for (b, start) in assign:
        pre = NP * 128
        xa = np.zeros((pre + seg, D), np.float32)
        lo = max(0, start - pre)
        xa[pre - (start - lo):pre] = x[b, lo:start]
        xa[pre:] = x[b, start:start + seg]
        pos = np.zeros((NM + 1) * 128, np.int32)
        if start >= 128:
            pos[0:128] = positions[b, start - 128:start]
        pos[128:] = positions[b, start:start + seg]
        m = dict(wmaps)
        m["x_all"] = xa
        m["pos"] = np.ascontiguousarray(pos.reshape(NM + 1, 128).T)
        for k in ("ident", "tri_inc", "tri_exc", "gmask", "mask", "invf", "sgn"):
            m[k] = consts[k]
        m["mask0"] = consts["mask"] if (start > 0 and NP > 0) else consts["mask_first"]
        maps.append(m)
    return maps


_NC_CACHE = {}


def kernel(x, positions, w_in, gla_w_gate_up, gla_b_gate, attn_sinks, gla_norm_w, w_out, ln_g, ln_b):
    NM = HALF // 128
    NP = HALF // 128
    key = (NP, NM)
    if key not in _NC_CACHE:
        _NC_CACHE[key] = build_nc(NP, NM)
    nc = _NC_CACHE[key]
    wmaps = _weight_maps(w_in, gla_w_gate_up, gla_b_gate, attn_sinks, gla_norm_w, w_out, ln_g, ln_b)
    assign = [(c // 2, (c % 2) * HALF) for c in range(NCORES)]
    maps = make_in_maps(x, positions, wmaps, NP, NM, assign)
    res = run_bass_kernel_spmd(nc, maps, core_ids=list(range(NCORES)))
    out = np.empty((BATCH, SEQ, D), np.float32)
    for c in range(NCORES):
        b, sg_ = c // 2, c % 2
        out[b, sg_ * HALF:(sg_ + 1) * HALF] = np.asarray(res.results[c]["y"], np.float32)
    return out
```
